# Optimizing a Trainium2 kernel written in Bass

```python
import jax, jax.numpy as jnp
from jax import lax
import numpy as np

D_MODEL = 1024
BATCH = 8
SEQ = 4096
DEPTH = 4

GRID_W = 64
CTX_LEN = 256
D_FF = 2752
N_MOD = 9
EPS = 1e-6
D_FOURIER = D_MODEL // 2
FOURIER_CH = 128
FOURIER_GROUPS = D_FOURIER // FOURIER_CH
D_SGU = D_MODEL // 2
SGU_CH = 128
SGU_GROUPS = D_SGU // SGU_CH
CHUNK = 128
HEAD_DIM = 128
N_HEADS = D_MODEL // HEAD_DIM
N_KV_HEADS = 2
GROUP = N_HEADS // N_KV_HEADS
ROPE_AXIS_DIM = HEAD_DIM // 2
ROPE_THETA = 10000.0
Q_BLOCK = 128

kernel_name = "hybrid_fnet_gmlp_gqa_dit_prefix"


def rms_norm(x, g):
    xf = x.astype(jnp.float32)
    y = xf * lax.rsqrt(jnp.mean(jnp.square(xf), axis=-1, keepdims=True) + EPS)
    return (y * g.astype(jnp.float32)).astype(x.dtype)


def adaln(cvec, w_mod, b_mod):
    m = jax.nn.silu(cvec) @ w_mod + b_mod
    m = m.reshape(m.shape[0], 1, N_MOD, D_MODEL)
    return [m[:, :, i] for i in range(N_MOD)]


def modulated_norm(x, g, shift, scale):
    return rms_norm(x, g) * (1 + scale) + shift


def swiglu(h, w_gu, w_down):
    gate, up = jnp.split(h @ w_gu, 2, axis=-1)
    return (jax.nn.silu(gate) * up) @ w_down


def ffn_half_step(x, mods, g, w_gu, w_down):
    shift, scale, gate = mods
    return x + 0.5 * gate * swiglu(modulated_norm(x, g, shift, scale), w_gu, w_down)


def fourier_mix(a):
    b_, n, _ = a.shape
    a4 = a.reshape(b_, n, FOURIER_GROUPS, FOURIER_CH).astype(jnp.float32)
    f = jnp.fft.fftn(a4, axes=(1, 3), norm="ortho").real
    return f.reshape(b_, n, D_FOURIER).astype(a.dtype)


def chunk_sgu(u, v, g_v, w_s, b_s):
    b_, n, _ = v.shape
    vh = rms_norm(v.reshape(b_, n, SGU_GROUPS, SGU_CH), g_v)
    vc = vh.reshape(b_, n // CHUNK, CHUNK, SGU_GROUPS, SGU_CH)
    mixed = jnp.einsum('hpq,bcqhd->bcphd', w_s, vc) + b_s.T[None, None, :, :, None]
    return u * mixed.reshape(b_, n, D_SGU)


def fourier_sgu_mixer(h, w_in, g_v, w_s, b_s, w_out):
    a, uv = jnp.split(h @ w_in, [D_FOURIER], axis=-1)
    u, v = jnp.split(jax.nn.gelu(uv), 2, axis=-1)
    out = jnp.concatenate([fourier_mix(a), chunk_sgu(u, v, g_v, w_s, b_s)], axis=-1)
    return out @ w_out


def axial_angles(rows):
    row = jnp.repeat(jnp.arange(rows, dtype=jnp.float32), GRID_W)
    col = jnp.tile(jnp.arange(GRID_W, dtype=jnp.float32), rows)
    inv_freq = ROPE_THETA ** (-jnp.arange(0, ROPE_AXIS_DIM, 2, dtype=jnp.float32) / ROPE_AXIS_DIM)
    return row[:, None] * inv_freq, col[:, None] * inv_freq


def rotate(x, ang):
    full = jnp.concatenate([ang, ang], axis=-1)[:, None, :]
    cos = jnp.cos(full).astype(x.dtype)
    sin = jnp.sin(full).astype(x.dtype)
    x1, x2 = jnp.split(x, 2, axis=-1)
    return x * cos + jnp.concatenate([-x2, x1], axis=-1) * sin


def axial_rope(x, ang_row, ang_col):
    xr, xc = jnp.split(x, 2, axis=-1)
    return jnp.concatenate([rotate(xr, ang_row), rotate(xc, ang_col)], axis=-1)


def project_kv(kv, g_k):
    b_, n, _ = kv.shape
    k, v = jnp.split(kv, 2, axis=-1)
    k = rms_norm(k.reshape(b_, n, N_KV_HEADS, HEAD_DIM), g_k)
    return k, v.reshape(b_, n, N_KV_HEADS, HEAD_DIM)


def project_q(q, g_q):
    b_, n, _ = q.shape
    return rms_norm(q.reshape(b_, n, N_HEADS, HEAD_DIM), g_q)


def gqa_attend(q, k, v):
    b_, nq = q.shape[:2]
    qg = q.reshape(b_, nq, N_KV_HEADS, GROUP, HEAD_DIM)
    s = jnp.einsum('bqkgd,bskd->bkgqs', qg, k, preferred_element_type=jnp.float32) * (HEAD_DIM ** -0.5)
    p = jax.nn.softmax(s, axis=-1).astype(v.dtype)
    o = jnp.einsum('bkgqs,bskd->bqkgd', p, v)
    return o.reshape(b_, nq, N_HEADS * HEAD_DIM)


def blocked_attention(q, k_all, v_all):
    b_, n = q.shape[:2]
    n_blk = n // Q_BLOCK
    qb = jnp.moveaxis(q.reshape(b_, n_blk, Q_BLOCK, N_HEADS, HEAD_DIM), 1, 0)
    o = lax.map(lambda qblk: gqa_attend(qblk, k_all, v_all), qb)
    return jnp.moveaxis(o, 0, 1).reshape(b_, n, N_HEADS * HEAD_DIM)


def setup_inputs(seed: int = 0) -> dict:
    key = jax.random.key(seed)
    ks = jax.random.split(key, 24)
    n_even = (DEPTH + 1) // 2
    n_odd = DEPTH // 2

    def nrm(k, shape, fan_in, s=1.0):
        return s * jax.random.normal(k, shape, jnp.float32) * (fan_in ** -0.5)

    def gain(k, shape):
        return 1.0 + 0.05 * jax.random.normal(k, shape, jnp.float32)

    return {
        "x": jax.random.normal(ks[0], (BATCH, SEQ, D_MODEL), jnp.float32),
        "c": jax.random.normal(ks[1], (BATCH, D_MODEL), jnp.float32),
        "ctx": jax.random.normal(ks[2], (BATCH, CTX_LEN, D_MODEL), jnp.float32),
        "c_ctx": jax.random.normal(ks[3], (D_MODEL,), jnp.float32),
        "w_mod": nrm(ks[4], (DEPTH, D_MODEL, N_MOD * D_MODEL), D_MODEL, 0.5),
        "b_mod": 0.02 * jax.random.normal(ks[5], (DEPTH, N_MOD * D_MODEL), jnp.float32),
        "g_ffn1": gain(ks[6], (DEPTH, D_MODEL)),
        "w_ffn1_gu": nrm(ks[7], (DEPTH, D_MODEL, 2 * D_FF), D_MODEL),
        "w_ffn1_down": nrm(ks[8], (DEPTH, D_FF, D_MODEL), D_FF),
        "g_mix": gain(ks[9], (DEPTH, D_MODEL)),
        "g_ffn2": gain(ks[10], (DEPTH, D_MODEL)),
        "w_ffn2_gu": nrm(ks[11], (DEPTH, D_MODEL, 2 * D_FF), D_MODEL),
        "w_ffn2_down": nrm(ks[12], (DEPTH, D_FF, D_MODEL), D_FF),
        "g_final": gain(ks[13], (D_MODEL,)),
        "w_in_ab": nrm(ks[14], (n_even, D_MODEL, D_FOURIER + 2 * D_SGU), D_MODEL),
        "g_v": gain(ks[15], (n_even, SGU_GROUPS, SGU_CH)),
        "w_s": nrm(ks[16], (n_even, SGU_GROUPS, CHUNK, CHUNK), CHUNK, 0.5),
        "b_s": 1.0 + 0.02 * jax.random.normal(ks[17], (n_even, SGU_GROUPS, CHUNK), jnp.float32),
        "w_out_ab": nrm(ks[18], (n_even, D_FOURIER + D_SGU, D_MODEL), D_FOURIER + D_SGU),
        "w_qkv": nrm(ks[19], (n_odd, D_MODEL, (N_HEADS + 2 * N_KV_HEADS) * HEAD_DIM), D_MODEL),
        "g_q": gain(ks[20], (n_odd, HEAD_DIM)),
        "g_k": gain(ks[21], (n_odd, HEAD_DIM)),
        "w_o": nrm(ks[22], (n_odd, N_HEADS * HEAD_DIM, D_MODEL), N_HEADS * HEAD_DIM),
    }


def reference(x, c, ctx, c_ctx, w_mod, b_mod, g_ffn1, w_ffn1_gu, w_ffn1_down, g_mix,
              g_ffn2, w_ffn2_gu, w_ffn2_down, g_final, w_in_ab, g_v, w_s, b_s, w_out_ab,
              w_qkv, g_q, g_k, w_o):
    n_tok = x.shape[1]
    ROWS = n_tok // GRID_W
    ang_row, ang_col = axial_angles(ROWS)
    q_cols = N_HEADS * HEAD_DIM

    for l in range(DEPTH):
        last = l == DEPTH - 1
        even = l % 2 == 0
        j = l // 2
        mx = adaln(c, w_mod[l], b_mod[l])
        mc = adaln(c_ctx[None], w_mod[l], b_mod[l])
        ctx_live = (not last) or (not even)

        x = ffn_half_step(x, mx[0:3], g_ffn1[l], w_ffn1_gu[l], w_ffn1_down[l])
        if ctx_live:
            ctx = ffn_half_step(ctx, mc[0:3], g_ffn1[l], w_ffn1_gu[l], w_ffn1_down[l])

        hx = modulated_norm(x, g_mix[l], mx[3], mx[4])
        if even:
            out_x = fourier_sgu_mixer(hx, w_in_ab[j], g_v[j], w_s[j], b_s[j], w_out_ab[j])
            if not last:
                hc = modulated_norm(ctx, g_mix[l], mc[3], mc[4])
                out_c = fourier_sgu_mixer(hc, w_in_ab[j], g_v[j], w_s[j], b_s[j], w_out_ab[j])
        else:
            hc = modulated_norm(ctx, g_mix[l], mc[3], mc[4])
            qkv_x = hx @ w_qkv[j]
            q_x = axial_rope(project_q(qkv_x[..., :q_cols], g_q[j]), ang_row, ang_col)
            k_x, v_x = project_kv(qkv_x[..., q_cols:], g_k[j])
            k_x = axial_rope(k_x, ang_row, ang_col)
            if last:
                k_c, v_c = project_kv(hc @ w_qkv[j][:, q_cols:], g_k[j])
            else:
                qkv_c = hc @ w_qkv[j]
                k_c, v_c = project_kv(qkv_c[..., q_cols:], g_k[j])
                q_c = project_q(qkv_c[..., :q_cols], g_q[j])
                out_c = gqa_attend(q_c, k_c, v_c) @ w_o[j]
            k_all = jnp.concatenate([k_x, k_c], axis=1)
            v_all = jnp.concatenate([v_x, v_c], axis=1)
            out_x = blocked_attention(q_x, k_all, v_all) @ w_o[j]

        x = x + mx[5] * out_x
        x = ffn_half_step(x, mx[6:9], g_ffn2[l], w_ffn2_gu[l], w_ffn2_down[l])
        if not last:
            ctx = ctx + mc[5] * out_c
            ctx = ffn_half_step(ctx, mc[6:9], g_ffn2[l], w_ffn2_gu[l], w_ffn2_down[l])

    return rms_norm(x, g_final)
```

```python
import numpy as np
from contextlib import ExitStack
import ml_dtypes
import concourse.bass as bass
import concourse.mybir as mybir
from concourse.bass_utils import run_bass_kernel_spmd

F32 = mybir.dt.float32
BF16 = mybir.dt.bfloat16
AF = mybir.ActivationFunctionType
ALU = mybir.AluOpType
AX = mybir.AxisListType

D = 1024
DFF = 2752
NMOD = 9
EPS = 1e-6
KC = 8
FCH = 22
GRID_W = 64
TC = 256
HD = 128
NH = 8
NKV = 2


class Res:
    __slots__ = ("w", "r")

    def __init__(self):
        self.w = None
        self.r = {}


def mkres(n):
    return [Res() for _ in range(n)]


class Chan:
    def __init__(self, sem):
        self.sem = sem
        self.cnt = 0


class Prog:
    def __init__(self, nc, es):
        self.nc = nc
        self.es = es
        self.eng = {}
        for name, e in (("pe", nc.tensor), ("act", nc.scalar), ("dve", nc.vector),
                        ("pool", nc.gpsimd), ("sp", nc.sync)):
            sem = es.enter_context(nc.semaphore("s_" + name))
            self.eng[name] = dict(e=e, sem=sem, cnt=0, seen={})
        self.pool_ch = []
        self.ch_kind = {}
        self.ch_kidx = {}

    def chan(self, kind="sp"):
        lst = self.ch_kind.setdefault(kind, [])
        idx = self.ch_kidx.get(kind, 0)
        if idx == len(lst):
            sem = self.es.enter_context(self.nc.semaphore("c%s%d" % (kind, idx)))
            c = Chan(sem)
            lst.append(c)
            self.pool_ch.append(c)
        self.ch_kidx[kind] = idx + 1
        return lst[idx]

    def _wait(self, en, ev):
        sem, val, owner = ev
        if owner == en and en == "pe":
            return
        E = self.eng[en]
        k = id(sem)
        if E["seen"].get(k, 0) >= val:
            return
        E["e"].wait_ge(sem, val)
        E["seen"][k] = val

    def _deps(self, en, reads, writes):
        evs = {}

        def add(ev):
            if ev is None:
                return
            k = id(ev[0])
            if k not in evs or evs[k][1] < ev[1]:
                evs[k] = ev
        for r in reads:
            add(r.w)
        for r in writes:
            add(r.w)
            for ev in r.r.values():
                add(ev)
        for ev in evs.values():
            self._wait(en, ev)

    @staticmethod
    def _commit(ev, reads, writes):
        k = id(ev[0])
        for r in reads:
            r.r[k] = ev
        for r in writes:
            r.w = ev
            r.r = {}

    def op(self, en, fn, reads=(), writes=()):
        self._deps(en, reads, writes)
        E = self.eng[en]
        ins = fn(E["e"])
        E["cnt"] += 1
        ins.then_inc(E["sem"], 1)
        ev = (E["sem"], E["cnt"], en)
        self._commit(ev, reads, writes)
        return ev

    def dma(self, en, ch, out, in_, reads=(), writes=()):
        self._deps(en, reads, writes)
        E = self.eng[en]
        ins = E["e"].dma_start(out=out, in_=in_)
        ch.cnt += 16
        ins.then_inc(ch.sem, 16)
        ev = (ch.sem, ch.cnt, None)
        self._commit(ev, reads, writes)
        return ev

    def barrier(self):
        sp = self.eng["sp"]
        for name, E in self.eng.items():
            if name != "sp" and E["cnt"] > 0:
                self._wait("sp", (E["sem"], E["cnt"], name))
        for c in self.pool_ch:
            if c.cnt:
                self._wait("sp", (c.sem, c.cnt, None))
        sp["cnt"] += 1
        sp["e"].sem_inc(sp["sem"], 1)
        ev = (sp["sem"], sp["cnt"], "sp")
        for name in self.eng:
            if name != "sp":
                self._wait(name, ev)
        for name, E in self.eng.items():
            for n2, E2 in self.eng.items():
                E["seen"][id(E2["sem"])] = E2["cnt"]
            for c in self.pool_ch:
                E["seen"][id(c.sem)] = c.cnt
        self.ch_kidx = {}


class G:
    pass


class Seq:
    def __init__(self, ap, T, s, nt):
        self.ap = ap
        self.T = T
        self.s = s
        self.nt = nt
        self.ntiles = T // nt
        self.res = mkres(self.ntiles)

    def tile_ap(self, i, c0=0, c1=KC):
        return self.ap[c0 * 128:c1 * 128, i * self.nt:(i + 1) * self.nt].rearrange(
            "(k p) t -> p k t", p=128)


_UID = [0]
SKIP = set()


def sb(es, nc, name, shape, dt):
    _UID[0] += 1
    return es.enter_context(nc.sbuf_tensor("%s_%d" % (name, _UID[0]), shape, dt))


def prologue(P, g):
    nc = P.nc
    L = g.L
    R_c = Res()
    P.op("dve", lambda e: e.memset(g.ones_dm[:, :], 1.0 / D), writes=[R_c])
    P.op("dve", lambda e: e.memset(g.ones_hd[:, :], 1.0 / HD), writes=[R_c])
    P.op("dve", lambda e: e.memset(g.ones_1[:, :], 1.0), writes=[R_c])
    P.op("dve", lambda e: e.memset(g.one11[:, :], 1.0), writes=[R_c])
    P.op("dve", lambda e: e.memset(g.ones_f[:, :], 1.0), writes=[R_c])
    ch = P.chan()
    P.dma("sp", ch, g.ident[:, :], g.d_ident[:, :], writes=[R_c])
    P.dma("sp", ch, g.rotm_f[:, :], g.d_rot[:, :], writes=[R_c])
    P.dma("sp", ch, g.dftc_f[:, :, :], g.d_dftc.rearrange("a p n -> p a n"), writes=[R_c])
    P.op("dve", lambda e: e.tensor_copy(out=g.rotm[:, :], in_=g.rotm_f[:, :]), reads=[R_c], writes=[R_c])
    P.op("dve", lambda e: e.tensor_copy(out=g.dftc[:, :, :], in_=g.dftc_f[:, :, :]), reads=[R_c], writes=[R_c])
    P.dma("sp", ch, g.nyq_f[:, :], g.d_nyq[:, :], writes=[R_c])
    P.op("dve", lambda e: e.tensor_copy(out=g.nyq[:, :], in_=g.nyq_f[:, :]), reads=[R_c], writes=[R_c])
    with ExitStack() as es:
        crow = sb(es, nc, "crow", [1, 2, D], F32)
        grow = sb(es, nc, "grow", [1, 3 * L + 1, D], F32)
        qkrow = sb(es, nc, "qkrow", [1, 4, HD], F32)
        brow = sb(es, nc, "brow", [1, NMOD * D], F32)
        R_brow = Res()
        ch_b = P.chan()
        sc = sb(es, nc, "sc", [128, KC, 2], F32)
        NP = 9
        PW = NMOD * D // NP
        wm = [sb(es, nc, "wm%d" % i, [128, KC, PW], BF16) for i in range(2)]
        scb = sb(es, nc, "scb", [128, KC, 2], BF16)
        R_rows = Res()
        ch2 = P.chan()
        P.dma("sp", ch2, crow[0:1, 0, :], g.d_c[0:1, :], writes=[R_rows])
        P.dma("sp", ch2, crow[0:1, 1, :], g.d_cctx[0:1, :], writes=[R_rows])
        for n, dt_ in enumerate((g.d_gffn1, g.d_gmix, g.d_gffn2)):
            P.dma("sp", ch2, grow[0:1, n * L:(n + 1) * L, :], dt_[0:1, :, :], writes=[R_rows])
        P.dma("sp", ch2, grow[0:1, 3 * L, :], g.d_gfinal[0:1, :], writes=[R_rows])
        n_odd = max(L // 2, 1)
        P.op("dve", lambda e: e.memset(qkrow[0:1, :, :], 1.0), writes=[R_rows])
        P.dma("sp", ch2, qkrow[0:1, 0:n_odd, :], g.d_gq[0:1, :, :], writes=[R_rows])
        P.dma("sp", ch2, qkrow[0:1, 2:2 + n_odd, :], g.d_gk[0:1, :, :], writes=[R_rows])
        ps = g.ps
        Rp = g.Rps
        for k in range(KC):
            for s in range(2):
                P.op("pe", lambda e, k=k, s=s: e.matmul(ps[0][:, 2 * k + s:2 * k + s + 1],
                                                       lhsT=crow[0:1, s, k * 128:(k + 1) * 128],
                                                       rhs=g.one11[0:1, 0:1], start=True, stop=True),
                     reads=[R_rows, R_c], writes=[Rp[0]])
        R_sc = Res()
        P.op("act", lambda e: e.activation(out=sc[:, :, :].rearrange("p k s -> p (k s)"), in_=ps[0][:, 0:16],
                                           func=AF.Silu), reads=[Rp[0]], writes=[R_sc])
        P.op("dve", lambda e: e.tensor_copy(out=scb[:, :, :], in_=sc[:, :, :]), reads=[R_sc], writes=[R_sc])
        ng = 3 * L + 1
        for n in range(ng):
            for k in range(KC):
                P.op("pe", lambda e, n=n, k=k: e.matmul(ps[1][:, n * KC + k:n * KC + k + 1],
                                                       lhsT=grow[0:1, n, k * 128:(k + 1) * 128],
                                                       rhs=g.one11[0:1, 0:1], start=True, stop=True),
                     reads=[R_rows, R_c], writes=[Rp[1]])
        for n in range(4):
            P.op("pe", lambda e, n=n: e.matmul(ps[1][:, ng * KC + n:ng * KC + n + 1],
                                               lhsT=qkrow[0:1, n, :], rhs=g.one11[0:1, 0:1],
                                               start=True, stop=True),
                 reads=[R_rows, R_c], writes=[Rp[1]])
        P.op("dve", lambda e: e.tensor_copy(out=g.gT[:, :, :].rearrange("p n k -> p (n k)"),
                                            in_=ps[1][:, 0:ng * KC]), reads=[Rp[1]], writes=[g.R_mods])
        P.op("dve", lambda e: e.tensor_copy(out=g.gqk[:, 0:4], in_=ps[1][:, ng * KC:ng * KC + 4]),
             reads=[Rp[1]], writes=[g.R_mods])
        P.op("dve", lambda e: e.tensor_scalar(out=g.gqk[:, 4:6], in0=g.gqk[:, 0:2], scalar1=float(HD) ** -0.5,
                                              scalar2=None, op0=ALU.mult), reads=[g.R_mods], writes=[g.R_mods])
        R_wm = mkres(2)
        chw = [P.chan("pool"), P.chan("pool")]
        pi = 0
        for l in range(L):
            P.dma("sp", ch_b, brow[0:1, :], g.d_bmod[0:1, l, :], writes=[R_brow])
            for pc in range(NP):
                slot = pi % 2
                P.dma("pool", chw[slot], wm[slot][:, :, :],
                      g.d_wmod[l, :, pc * PW:(pc + 1) * PW].rearrange("(k p) n -> p k n", p=128),
                      writes=[R_wm[slot]])
                bank = 2 + slot
                for cc in range(PW // 128):
                    gc = pc * (PW // 128) + cc
                    for k in range(KC):
                        P.op("pe", lambda e, k=k, cc=cc, slot=slot, bank=bank: e.matmul(
                            ps[bank][:, 2 * cc:2 * cc + 2], lhsT=wm[slot][:, k, cc * 128:(cc + 1) * 128],
                            rhs=scb[:, k, :], start=(k == 0), stop=False),
                            reads=[R_wm[slot], R_sc], writes=[Rp[bank]])
                    P.op("pe", lambda e, cc=cc, gc=gc, l=l, bank=bank: e.matmul(
                        ps[bank][:, 2 * cc:2 * cc + 2], lhsT=brow[0:1, gc * 128:(gc + 1) * 128],
                        rhs=g.one11[0:1, 0:2], start=False, stop=True),
                        reads=[R_brow, R_c], writes=[Rp[bank]])
                nch = PW // 128
                P.op("dve", lambda e, l=l, pc=pc, bank=bank, nch=nch: e.tensor_copy(
                    out=g.mods[:, l, pc * nch:(pc + 1) * nch, :].rearrange("p c s -> p (c s)"),
                    in_=ps[bank][:, 0:2 * nch]), reads=[Rp[bank]], writes=[g.R_mods])
                pi += 1
        for l in range(L):
            for n in range(3):
                sh_i, sc_i, gt_i = 3 * n, 3 * n + 1, 3 * n + 2
                P.op("dve", lambda e, l=l, n=n, sc_i=sc_i: e.tensor_scalar(
                    out=g.GS[:, l, n, :, :], in0=g.mods[:, l, sc_i * KC:(sc_i + 1) * KC, :], scalar1=1.0,
                    scalar2=None, op0=ALU.add), reads=[g.R_mods], writes=[g.R_mods])
                P.op("dve", lambda e, l=l, n=n: e.tensor_tensor(
                    out=g.GS[:, l, n, :, :], in0=g.GS[:, l, n, :, :],
                    in1=g.gT[:, n * L + l, :].unsqueeze(2).broadcast_to([128, KC, 2]), op=ALU.mult),
                    reads=[g.R_mods], writes=[g.R_mods])
                P.op("dve", lambda e, l=l, n=n, gt_i=gt_i: e.tensor_scalar(
                    out=g.GT[:, l, n, :, :], in0=g.mods[:, l, gt_i * KC:(gt_i + 1) * KC, :],
                    scalar1=(1.0 if n == 1 else 0.5), scalar2=None, op0=ALU.mult),
                    reads=[g.R_mods], writes=[g.R_mods])
        P.barrier()


def shift_ap(g, l, n, c, s):
    i = 3 * n
    return g.mods[:, l, i * KC + c, s:s + 1]


def to_feature_major(P, g, src, seq):
    nc = P.nc
    with ExitStack() as es:
        xin = [sb(es, nc, "xin%d" % i, [128, D], F32) for i in range(3)]
        xo = [sb(es, nc, "xo%d" % i, [128, KC, 512], F32) for i in range(2)]
        R_in = mkres(3)
        R_o = mkres(2)
        ch_in = [P.chan() for _ in range(3)]
        ch_o = [P.chan("pool") for _ in range(2)]
        nq = seq.nt // 128
        cnt = 0
        for i in range(seq.ntiles):
            so = i % 2
            for q in range(nq):
                si = cnt % 3
                cnt += 1
                r0 = i * seq.nt + q * 128
                P.dma("sp", ch_in[si], xin[si][:, :], src[r0:r0 + 128, :], writes=[R_in[si]])
                for c in range(KC):
                    P.op("pe", lambda e, c=c, q=q, si=si: e.transpose(
                        out=g.ps[c][:, q * 128:(q + 1) * 128], in_=xin[si][:, c * 128:(c + 1) * 128],
                        identity=g.ident[:, :]), reads=[R_in[si]], writes=[g.Rps[c]])
            for c in range(KC):
                en = "dve" if c % 2 == 0 else "act"
                if en == "dve":
                    P.op("dve", lambda e, c=c, so=so: e.tensor_copy(out=xo[so][:, c, :seq.nt], in_=g.ps[c][:, :seq.nt]),
                         reads=[g.Rps[c]], writes=[R_o[so]])
                else:
                    P.op("act", lambda e, c=c, so=so: e.copy(out=xo[so][:, c, :seq.nt], in_=g.ps[c][:, :seq.nt]),
                         reads=[g.Rps[c]], writes=[R_o[so]])
            P.dma("pool", ch_o[so], seq.tile_ap(i), xo[so][:, :, :seq.nt], reads=[R_o[so]], writes=[seq.res[i]])
        P.barrier()


def emit_norm_stats(P, g, xt, R_x, nt, sq, R_sq, bank, rstd, R_rstd, sd, R_sd):
    for c in range(KC):
        s = c % 2
        P.op("act", lambda e, c=c, s=s: e.activation(out=sq[s][:, :nt], in_=xt[:, c, :nt], func=AF.Square),
             reads=[R_x[c]], writes=[R_sq[s]])
        P.op("pe", lambda e, c=c, s=s: e.matmul(g.ps[bank][:, :nt], lhsT=g.ones_dm[:, :], rhs=sq[s][:, :nt],
                                                start=(c == 0), stop=(c == KC - 1)),
             reads=[R_sq[s]], writes=[g.Rps[bank]])
    P.op("act", lambda e: e.activation(out=sd[:, :nt], in_=g.ps[bank][:, :nt], func=AF.Ln, bias=g.eps_t[:, 0:1],
                                       scale=1.0), reads=[g.Rps[bank]], writes=[R_sd])
    P.op("act", lambda e: e.activation(out=rstd[:, :nt], in_=sd[:, :nt], func=AF.Exp, scale=-0.5),
         reads=[R_sd], writes=[R_rstd])


def emit_modnorm(P, g, l, n, s, xt, R_x, nt, hT, hoff, R_h, W, sq_in_h=False):
    if sq_in_h:
        bank = W["bank"]
        for c in range(KC):
            P.op("act", lambda e, c=c: e.activation(out=hT[:, c, hoff:hoff + nt], in_=xt[:, c, :nt], func=AF.Square),
                 reads=[R_x[c]], writes=[R_h[c]])
        for c in range(KC):
            P.op("pe", lambda e, c=c: e.matmul(g.ps[bank][:, :nt], lhsT=g.ones_dm[:, :], rhs=hT[:, c, hoff:hoff + nt],
                                               start=(c == 0), stop=(c == KC - 1)),
                 reads=[R_h[c]], writes=[g.Rps[bank]])
        P.op("act", lambda e: e.activation(out=W["sd"][:, :nt], in_=g.ps[bank][:, :nt], func=AF.Ln,
                                           bias=g.eps_t[:, 0:1], scale=1.0), reads=[g.Rps[bank]], writes=[W["R_sd"]])
        P.op("act", lambda e: e.activation(out=W["rstd"][:, :nt], in_=W["sd"][:, :nt], func=AF.Exp, scale=-0.5),
             reads=[W["R_sd"]], writes=[W["R_rstd"]])
    else:
        emit_norm_stats(P, g, xt, R_x, nt, W["sq"], W["R_sq"], W["bank"], W["rstd"], W["R_rstd"], W["sd"], W["R_sd"])
    for c in range(KC):
        ts = c % 2
        tb = W["t"][ts]
        P.op("dve", lambda e, c=c, tb=tb: e.scalar_tensor_tensor(
            out=tb[:, :nt], in0=xt[:, c, :nt], scalar=g.GS[:, l, n, c, s:s + 1], in1=W["rstd"][:, :nt],
            op0=ALU.mult, op1=ALU.mult), reads=[R_x[c], W["R_rstd"], g.R_mods], writes=[W["R_t"][ts]])
        P.op("act", lambda e, c=c, tb=tb: e.activation(
            out=hT[:, c, hoff:hoff + nt], in_=tb[:, :nt], func=AF.Identity, bias=shift_ap(g, l, n, c, s),
            scale=1.0), reads=[W["R_t"][ts], g.R_mods], writes=[R_h[c]])


def alloc_norm_work(es, nc, bank):
    W = {}
    W["sq"] = [sb(es, nc, "nsq%d" % i, [128, 512], BF16) for i in range(2)]
    W["R_sq"] = mkres(2)
    W["t"] = [sb(es, nc, "nt%d" % i, [128, 512], F32) for i in range(2)]
    W["R_t"] = mkres(2)
    W["rstd"] = sb(es, nc, "nrstd", [128, 512], F32)
    W["R_rstd"] = Res()
    W["sd"] = sb(es, nc, "nsd", [128, 512], F32)
    W["R_sd"] = Res()
    W["bank"] = bank
    return W


def final_phase(P, g, seq, dst):
    nc = P.nc
    L = g.L
    with ExitStack() as es:
        xt = [sb(es, nc, "fxt%d" % i, [128, KC, 512], F32) for i in range(2)]
        R_x = [mkres(KC) for _ in range(2)]
        yfs = [sb(es, nc, "fyf%d" % i, [128, KC, 512], F32) for i in range(2)]
        R_yfs = [mkres(KC) for _ in range(2)]
        yo = [sb(es, nc, "fyo%d" % i, [128, 2, D], F32) for i in range(2)]
        R_yo = mkres(2)
        sq = [sb(es, nc, "fsq%d" % i, [128, 512], BF16) for i in range(2)]
        R_sq = mkres(2)
        rstds = [sb(es, nc, "frstd%d" % i, [128, 512], F32) for i in range(2)]
        sds = [sb(es, nc, "fsd%d" % i, [128, 512], F32) for i in range(2)]
        R_rstds, R_sds = mkres(2), mkres(2)
        ch_x = [P.chan(), P.chan()]
        ch_o = [P.chan("pool"), P.chan("pool")]
        nt = seq.nt
        oc = 0
        for i in range(seq.ntiles):
            sx = i % 2
            yf, R_yf = yfs[sx], R_yfs[sx]
            rstd, R_rstd, sd, R_sd = rstds[sx], R_rstds[sx], sds[sx], R_sds[sx]
            P.dma("sp", ch_x[sx], xt[sx][:, :, :nt], seq.tile_ap(i), reads=[seq.res[i]], writes=R_x[sx])
            emit_norm_stats(P, g, xt[sx], R_x[sx], nt, sq, R_sq, 0, rstd, R_rstd, sd, R_sd)
            for c in range(KC):
                P.op("dve", lambda e, c=c, sx=sx: e.scalar_tensor_tensor(
                    out=yf[:, c, :nt], in0=xt[sx][:, c, :nt], scalar=g.gT[:, 3 * L, c:c + 1], in1=rstd[:, :nt],
                    op0=ALU.mult, op1=ALU.mult), reads=[R_x[sx][c], R_rstd, g.R_mods], writes=[R_yf[c]])
            for qq in range(nt // 256):
                so = oc % 2
                oc += 1
                for h in range(2):
                    q = qq * 2 + h
                    for c in range(KC):
                        bank = 1 + h * 2 + c // 4
                        P.op("pe", lambda e, c=c, q=q, bank=bank: e.transpose(
                            out=g.ps[bank][:, (c % 4) * 128:(c % 4 + 1) * 128], in_=yf[:, c, q * 128:(q + 1) * 128],
                            identity=g.ident[:, :]), reads=[R_yf[c]], writes=[g.Rps[bank]])
                    for hh in range(2):
                        bank = 1 + h * 2 + hh
                        if hh == 0:
                            P.op("dve", lambda e, h=h, hh=hh, bank=bank, so=so: e.tensor_copy(
                                out=yo[so][:, h, hh * 512:(hh + 1) * 512], in_=g.ps[bank][:, :]),
                                reads=[g.Rps[bank]], writes=[R_yo[so]])
                        else:
                            P.op("act", lambda e, h=h, hh=hh, bank=bank, so=so: e.copy(
                                out=yo[so][:, h, hh * 512:(hh + 1) * 512], in_=g.ps[bank][:, :]),
                                reads=[g.Rps[bank]], writes=[R_yo[so]])
                r0 = i * nt + qq * 256
                P.dma("pool", ch_o[so], dst[r0:r0 + 256, :].rearrange("(h p) f -> p h f", p=128), yo[so][:, :, :],
                      reads=[R_yo[so]], writes=[g.R_out])
        P.barrier()


def ffn_phase(P, g, jobs):
    nc = P.nc
    GW = 384
    groups = []
    c0 = 0
    while c0 < DFF:
        c1 = min(c0 + GW, DFF)
        groups.append((c0, c1))
        c0 = c1
    NG = len(groups)
    tiles = []
    sup_idx = []
    for jb, (l_, n_, _, _, supers) in enumerate(jobs):
        for S in supers:
            ids = []
            for jj, (seq, ti) in enumerate(S):
                ids.append(len(tiles))
                tiles.append((seq, ti, jj, jb))
            sup_idx.append((jb, ids))
    NT = len(tiles)
    NS = len(sup_idx)
    with ExitStack() as es:
        act = sb(es, nc, "f_act", [128, FCH, 1024], BF16)
        hT = sb(es, nc, "f_hT", [128, KC, 1024], BF16)
        wd = sb(es, nc, "f_wd", [128, FCH, D], BF16)
        xt = [sb(es, nc, "f_xt%d" % i, [128, KC, 512], F32) for i in range(3)]
        wgu = [sb(es, nc, "f_wgu%d" % i, [128, KC, 2, GW], BF16) for i in range(2)]
        sg = [sb(es, nc, "f_sg%d" % i, [128, 512], F32) for i in range(2)]
        W = alloc_norm_work(es, nc, 0)
        R_act = [mkres(2) for _ in range(FCH)]
        R_h = [mkres(KC) for _ in range(2)]
        R_x = [mkres(KC) for _ in range(3)]
        R_wd = mkres(4)
        R_wgu = mkres(2)
        R_sg = mkres(2)
        ch_x = [P.chan() for _ in range(3)]
        ch_st = [P.chan() for _ in range(3)]
        ch_wgu = [P.chan("pool"), P.chan("pool")]
        ch_wd = [P.chan("pool") for _ in range(4)]
        st = dict(gi=0, pcnt=0, ocnt=0)

        def load_group(jb, gidx, slot):
            l_, n_, d_wgu, d_wdown, _ = jobs[jb]
            a0, a1 = groups[gidx]
            w = a1 - a0
            P.dma("pool", ch_wgu[slot], wgu[slot][:, :, 0, 0:w],
                  d_wgu[l_, :, a0:a1].rearrange("(k p) n -> p k n", p=128), writes=[R_wgu[slot]])
            P.dma("pool", ch_wgu[slot], wgu[slot][:, :, 1, 0:w],
                  d_wgu[l_, :, DFF + a0:DFF + a1].rearrange("(k p) n -> p k n", p=128), writes=[R_wgu[slot]])

        def load_wdown(jb, qs=(0, 1, 2, 3)):
            l_, n_, d_wgu, d_wdown, _ = jobs[jb]
            for q in qs:
                P.dma("pool", ch_wd[q], wd[:, 0:FCH - 1, q * 256:(q + 1) * 256],
                      d_wdown[l_, 0:(FCH - 1) * 128, q * 256:(q + 1) * 256].rearrange("(c p) d -> p c d", p=128),
                      writes=[R_wd[q]])
                P.dma("pool", ch_wd[q], wd[0:64, FCH - 1, q * 256:(q + 1) * 256],
                      d_wdown[l_, (FCH - 1) * 128:DFF, q * 256:(q + 1) * 256], writes=[R_wd[q]])

        def load(t):
            seq, ti, jj, jb = tiles[t]
            s3 = t % 3
            P.dma("sp", ch_x[s3], xt[s3][:, :, :seq.nt], seq.tile_ap(ti), reads=[seq.res[ti]], writes=R_x[s3])

        def norm(t):
            seq, ti, jj, jb = tiles[t]
            s3 = t % 3
            emit_modnorm(P, g, jobs[jb][0], jobs[jb][1], seq.s, xt[s3], R_x[s3], seq.nt, hT, jj * 512, R_h[jj], W,
                         sq_in_h=True)

        def store(t):
            seq, ti, jj, jb = tiles[t]
            s3 = t % 3
            P.dma("sp", ch_st[s3], seq.tile_ap(ti), xt[s3][:, :, :seq.nt], reads=R_x[s3], writes=[seq.res[ti]])

        normed = set()

        def norm_stage(t, stage):
            seq, ti, jj, jb = tiles[t]
            l_, n_ = jobs[jb][0], jobs[jb][1]
            s3 = t % 3
            nt = seq.nt
            off = jj * 512
            if stage == 0:
                for c in range(KC):
                    P.op("act", lambda e, c=c: e.activation(out=hT[:, c, off:off + nt], in_=xt[s3][:, c, :nt],
                                                            func=AF.Square), reads=[R_x[s3][c]], writes=[R_h[jj][c]])
            elif stage == 1:
                for c in range(KC):
                    P.op("pe", lambda e, c=c: e.matmul(g.ps[0][:, :nt], lhsT=g.ones_dm[:, :], rhs=hT[:, c, off:off + nt],
                                                       start=(c == 0), stop=(c == KC - 1)),
                         reads=[R_h[jj][c]], writes=[g.Rps[0]])
                P.op("act", lambda e: e.activation(out=W["sd"][:, :nt], in_=g.ps[0][:, :nt], func=AF.Ln,
                                                   bias=g.eps_t[:, 0:1], scale=1.0), reads=[g.Rps[0]], writes=[W["R_sd"]])
                P.op("act", lambda e: e.activation(out=W["rstd"][:, :nt], in_=W["sd"][:, :nt], func=AF.Exp, scale=-0.5),
                     reads=[W["R_sd"]], writes=[W["R_rstd"]])
            else:
                for c in range(KC):
                    ts = c % 2
                    tb_ = W["t"][ts]
                    P.op("dve", lambda e, c=c: e.scalar_tensor_tensor(
                        out=tb_[:, :nt], in0=xt[s3][:, c, :nt], scalar=g.GS[:, l_, n_, c, seq.s:seq.s + 1],
                        in1=W["rstd"][:, :nt], op0=ALU.mult, op1=ALU.mult),
                        reads=[R_x[s3][c], W["R_rstd"], g.R_mods], writes=[W["R_t"][ts]])
                    P.op("act", lambda e, c=c: e.activation(
                        out=hT[:, c, off:off + nt], in_=tb_[:, :nt], func=AF.Identity,
                        bias=shift_ap(g, l_, n_, c, seq.s), scale=1.0),
                        reads=[W["R_t"][ts], g.R_mods], writes=[R_h[jj][c]])
                normed.add(t)

        def phaseA(t, slot, a0, a1):
            seq, ti, jj, jb = tiles[t]
            nt = seq.nt
            off = jj * 512
            for fo in range(0, a1 - a0, 128):
                fw = min(128, a1 - a0 - fo)
                f = (a0 + fo) // 128
                pb = st["pcnt"] % 2
                st["pcnt"] += 1
                gb, ub = 1 + pb, 3 + pb
                for k in range(KC):
                    P.op("pe", lambda e, k=k: e.matmul(
                        g.ps[gb][:fw, :nt], lhsT=wgu[slot][:, k, 0, fo:fo + fw],
                        rhs=hT[:, k, off:off + nt], start=(k == 0), stop=(k == KC - 1)),
                        reads=[R_wgu[slot], R_h[jj][k]], writes=[g.Rps[gb]])
                for k in range(KC):
                    P.op("pe", lambda e, k=k: e.matmul(
                        g.ps[ub][:fw, :nt], lhsT=wgu[slot][:, k, 1, fo:fo + fw],
                        rhs=hT[:, k, off:off + nt], start=(k == 0), stop=(k == KC - 1)),
                        reads=[R_wgu[slot], R_h[jj][k]], writes=[g.Rps[ub]])
                P.op("act", lambda e: e.activation(out=sg[pb][:fw, :nt], in_=g.ps[gb][:fw, :nt], func=AF.Silu),
                     reads=[g.Rps[gb]], writes=[R_sg[pb]])
                P.op("dve", lambda e: e.tensor_tensor(
                    out=act[:fw, f, off:off + nt], in0=sg[pb][:fw, :nt], in1=g.ps[ub][:fw, :nt],
                    op=ALU.mult), reads=[R_sg[pb], g.Rps[ub]], writes=[R_act[f][jj]])

        def phaseB(t, hook=None):
            seq, ti, jj, jb = tiles[t]
            l_, n_ = jobs[jb][0], jobs[jb][1]
            nt = seq.nt
            off = jj * 512
            s3 = t % 3
            for dch in range(KC):
                ob = 5 + (st["ocnt"] % 2)
                st["ocnt"] += 1
                for f in range(FCH):
                    fw = 128 if f < FCH - 1 else 64
                    P.op("pe", lambda e, f=f, fw=fw: e.matmul(
                        g.ps[ob][:, :nt], lhsT=wd[:fw, f, dch * 128:(dch + 1) * 128],
                        rhs=act[:fw, f, off:off + nt], start=(f == 0), stop=(f == FCH - 1)),
                        reads=[R_wd[dch // 2], R_act[f][jj]], writes=[g.Rps[ob]])
                P.op("dve", lambda e: e.scalar_tensor_tensor(
                    out=xt[s3][:, dch, :nt], in0=g.ps[ob][:, :nt], scalar=g.GT[:, l_, n_, dch, seq.s:seq.s + 1],
                    in1=xt[s3][:, dch, :nt], op0=ALU.mult, op1=ALU.add),
                    reads=[g.Rps[ob], R_x[s3][dch], g.R_mods], writes=[R_x[s3][dch]])
                if hook is not None and dch in hook:
                    hook[dch]()

        pend_wd = [(0, (q,)) for q in range(4)]
        load(0)
        load_group(0, 0, 0)
        norm(0)
        for t in range(1, min(3, NT)):
            load(t)
        for si, (jb, ids) in enumerate(sup_idx):
            nxt = sup_idx[si + 1][1] if si + 1 < NS else []
            for gidx in range(NG):
                slot = st["gi"] % 2
                st["gi"] += 1
                if gidx + 1 < NG:
                    load_group(jb, gidx + 1, (slot + 1) % 2)
                elif si + 1 < NS:
                    load_group(sup_idx[si + 1][0], 0, (slot + 1) % 2)
                if pend_wd:
                    load_wdown(*pend_wd.pop(0))
                a0, a1 = groups[gidx]
                phaseA(ids[0], slot, a0, a1)
                if len(ids) == 2:
                    if gidx == 0 and ids[1] not in normed:
                        norm(ids[1])
                        normed.add(ids[1])
                    phaseA(ids[1], slot, a0, a1)
            if len(ids) == 2:
                phaseB(ids[0])
                store(ids[0])
                if len(nxt) == 2:
                    load(nxt[1])
                hk = {}
                if nxt:
                    hk[0] = lambda: norm_stage(nxt[0], 0)
                    hk[1] = lambda: norm_stage(nxt[0], 1)
                    hk[2] = lambda: norm_stage(nxt[0], 2)
                    if len(nxt) == 2:
                        hk[4] = lambda: norm_stage(nxt[1], 0)
                        hk[5] = lambda: norm_stage(nxt[1], 1)
                        hk[6] = lambda: norm_stage(nxt[1], 2)
                phaseB(ids[1], hook=hk)
                store(ids[1])
                if si + 2 < NS:
                    load(sup_idx[si + 2][1][0])
            else:
                hk = {}
                if nxt:
                    hk[1] = lambda: norm_stage(nxt[0], 0)
                    hk[2] = lambda: norm_stage(nxt[0], 1)
                    hk[3] = lambda: norm_stage(nxt[0], 2)
                phaseB(ids[0], hook=hk)
                store(ids[0])
                if len(nxt) == 2:
                    load(nxt[1])
                if si + 2 < NS:
                    load(sup_idx[si + 2][1][0])
            if si + 1 < NS and sup_idx[si + 1][0] != jb:
                for q in range(4):
                    pend_wd.append((sup_idx[si + 1][0], (q,)))
        P.barrier()


def make_supers(seqx, seqc):
    supers = []
    for i in range(0, seqx.ntiles, 2):
        supers.append([(seqx, i + k) for k in range(min(2, seqx.ntiles - i))])
    if seqc is not None:
        supers.append([(seqc, 0)])
    return supers


def even_mixer(P, g, l, seqs):
    nc = P.nc
    j = l // 2
    with ExitStack() as es:
        wout = sb(es, nc, "e_wout", [128, KC, D], BF16)
        R_w = Res()
        chw = P.chan("pool")
        chw2 = P.chan("pool")
        wout_loaded = [False]
        for seq in seqs:
            T = seq.T
            nt = seq.nt
            ntk = T // 128
            with ExitStack() as es2:
                a_tok = sb(es2, nc, "e_atok", [128, ntk, 512], BF16)
                sgu = sb(es2, nc, "e_sgu", [128, 4, T], BF16)
                R_atok = mkres(ntk)
                R_sgu = mkres(seq.ntiles)
                with ExitStack() as es3:
                    win = sb(es3, nc, "e_win", [128, KC, 1536], BF16)
                    R_win = Res()
                    P.dma("pool", chw2, win[:, :, :], g.d_win[j, :, :].rearrange("(k p) n -> p k n", p=128),
                          writes=[R_win])
                    wsf = sb(es3, nc, "e_wsf", [128, 4, 128], F32)
                    wsT = sb(es3, nc, "e_wsT", [128, 4, 128], BF16)
                    gv_bc = sb(es3, nc, "e_gv", [128, 512], F32)
                    bs_bc = sb(es3, nc, "e_bs", [128, 4, 128], F32)
                    chs = P.chan()
                    R_s = Res()
                    P.dma("sp", chs, wsf[:, :, :], g.d_ws[j, :, :, :].rearrange("h p q -> p h q"), writes=[R_s])
                    P.dma("sp", chs, gv_bc[:, :], g.d_gv[j, :, :].rearrange("h c -> (h c)").partition_broadcast(128),
                          writes=[R_s])
                    P.dma("sp", chs, bs_bc[:, :, :].rearrange("p h q -> p (h q)"),
                          g.d_bs[j, :, :].rearrange("h c -> (h c)").partition_broadcast(128), writes=[R_s])
                    for h in range(4):
                        P.op("pe", lambda e, h=h: e.transpose(out=g.ps[7][:, h * 128:(h + 1) * 128], in_=wsf[:, h, :],
                                                              identity=g.ident[:, :]), reads=[R_s], writes=[g.Rps[7]])
                    P.op("dve", lambda e: e.tensor_copy(out=wsT[:, :, :].rearrange("p h q -> p (h q)"), in_=g.ps[7][:, :]),
                         reads=[g.Rps[7]], writes=[R_s])
                    xt1 = sb(es3, nc, "e_xt1", [128, KC, 512], F32)
                    R_x1 = mkres(KC)
                    ch_x1 = P.chan()
                    hT = [sb(es3, nc, "e_hT%d" % i, [128, KC, 512], BF16) for i in range(2)]
                    R_h = [mkres(KC) for _ in range(2)]
                    uT = [sb(es3, nc, "e_uT%d" % i, [128, 4, 512], F32) for i in range(2)]
                    R_u = [mkres(4) for _ in range(2)]
                    vg = [sb(es3, nc, "e_vg%d" % i, [128, 512], F32) for i in range(2)]
                    vsq = [sb(es3, nc, "e_vsq%d" % i, [128, 512], F32) for i in range(2)]
                    vn = [sb(es3, nc, "e_vn%d" % i, [128, 512], BF16) for i in range(2)]
                    ssq = [sb(es3, nc, "e_ssq%d" % i, [128, 4], F32) for i in range(2)]
                    rsv = [sb(es3, nc, "e_rsv%d" % i, [128, 4], F32) for i in range(2)]
                    tmpm = [sb(es3, nc, "e_tmpm%d" % i, [128, 4, 128], F32) for i in range(2)]
                    R_vg, R_vsq, R_vn, R_ssq, R_rsv, R_tmpm = (mkres(2) for _ in range(6))
                    W = alloc_norm_work(es3, nc, 0)
                    nq = nt // 128

                    def m1_load_norm(ti):
                        sx = ti % 2
                        P.dma("sp", ch_x1, xt1[:, :, :nt], seq.tile_ap(ti), reads=[seq.res[ti]], writes=R_x1)
                        emit_modnorm(P, g, l, 1, seq.s, xt1, R_x1, nt, hT[sx], 0, R_h[sx], W, sq_in_h=True)

                    def m1_u(ti):
                        sx = ti % 2
                        for h in range(4):
                            bank = 1 + h % 2
                            for k in range(KC):
                                P.op("pe", lambda e, k=k: e.matmul(
                                    g.ps[bank][:, :nt], lhsT=win[:, k, 512 + h * 128:512 + (h + 1) * 128],
                                    rhs=hT[sx][:, k, :nt], start=(k == 0), stop=(k == KC - 1)),
                                    reads=[R_win, R_h[sx][k]], writes=[g.Rps[bank]])
                            P.op("act", lambda e: e.activation(
                                out=uT[sx][:, h, :nt], in_=g.ps[bank][:, :nt], func=AF.Gelu_apprx_tanh),
                                reads=[g.Rps[bank]], writes=[R_u[sx][h]])

                    def m1_proj(ti, q, c):
                        sx = ti % 2
                        tk = ti * nq + q
                        for k in range(KC):
                            P.op("pe", lambda e, k=k: e.matmul(
                                g.ps[3][:, :], lhsT=hT[sx][:, k, q * 128:(q + 1) * 128], rhs=win[:, k, 0:512],
                                start=(k == 0), stop=(k == KC - 1)),
                                reads=[R_win, R_h[sx][k]], writes=[g.Rps[3]])
                        P.op("act", lambda e: e.copy(out=a_tok[:, tk, :], in_=g.ps[3][:, :]),
                             reads=[g.Rps[3]], writes=[R_atok[tk]])
                        vb = 4 if c == 0 else 6
                        for k in range(KC):
                            P.op("pe", lambda e, k=k: e.matmul(
                                g.ps[vb][:, :], lhsT=hT[sx][:, k, q * 128:(q + 1) * 128], rhs=win[:, k, 1024:1536],
                                start=(k == 0), stop=(k == KC - 1)),
                                reads=[R_win, R_h[sx][k]], writes=[g.Rps[vb]])
                        P.op("act", lambda e: e.activation(out=vg[c][:, :], in_=g.ps[vb][:, :], func=AF.Gelu_apprx_tanh),
                             reads=[g.Rps[vb]], writes=[R_vg[c]])
                        P.op("dve", lambda e: e.tensor_tensor(out=vsq[c][:, :], in0=vg[c][:, :], in1=vg[c][:, :], op=ALU.mult),
                             reads=[R_vg[c]], writes=[R_vsq[c]])
                        P.op("dve", lambda e: e.tensor_reduce(
                            out=ssq[c][:, :], in_=vsq[c][:, :].rearrange("p (a b) -> p a b", a=4), axis=AX.X, op=ALU.add),
                            reads=[R_vsq[c]], writes=[R_ssq[c]])
                        P.op("act", lambda e: e.activation(out=ssq[c][:, :], in_=ssq[c][:, :], func=AF.Sqrt,
                                                           bias=g.eps_t[:, 0:1], scale=1.0 / 128),
                             reads=[R_ssq[c]], writes=[R_ssq[c]])
                        P.op("dve", lambda e: e.reciprocal(out=rsv[c][:, :], in_=ssq[c][:, :]),
                             reads=[R_ssq[c]], writes=[R_rsv[c]])
                        P.op("dve", lambda e: e.tensor_tensor(
                            out=vsq[c][:, :].rearrange("p (a b) -> p a b", a=4),
                            in0=vg[c][:, :].rearrange("p (a b) -> p a b", a=4),
                            in1=rsv[c][:, :].unsqueeze(2).broadcast_to([128, 4, 128]), op=ALU.mult),
                            reads=[R_vg[c], R_rsv[c]], writes=[R_vsq[c]])
                        P.op("dve", lambda e: e.tensor_tensor(out=vn[c][:, :], in0=vsq[c][:, :], in1=gv_bc[:, :], op=ALU.mult),
                             reads=[R_vsq[c], R_s], writes=[R_vn[c]])

                    def m1_sgu(ti, q, c):
                        sx = ti % 2
                        tk = ti * nq + q
                        mb = 5 if c == 0 else 7
                        for h in range(4):
                            P.op("pe", lambda e, h=h: e.matmul(
                                g.ps[mb][:, h * 128:(h + 1) * 128], lhsT=vn[c][:, h * 128:(h + 1) * 128], rhs=wsT[:, h, :],
                                start=True, stop=True), reads=[R_vn[c], R_s], writes=[g.Rps[mb]])
                        P.op("dve", lambda e: e.tensor_tensor(
                            out=tmpm[c][:, :, :], in0=g.ps[mb][:, :].rearrange("p (h q) -> p h q", h=4), in1=bs_bc[:, :, :],
                            op=ALU.add), reads=[g.Rps[mb], R_s], writes=[R_tmpm[c]])
                        P.op("dve", lambda e: e.tensor_tensor(
                            out=sgu[:, :, tk * 128:(tk + 1) * 128], in0=tmpm[c][:, :, :],
                            in1=uT[sx][:, :, q * 128:(q + 1) * 128], op=ALU.mult),
                            reads=[R_tmpm[c]] + R_u[sx], writes=[R_sgu[ti]])

                    m1_load_norm(0)
                    if not wout_loaded[0]:
                        P.dma("pool", chw, wout[:, :, :], g.d_wout[j, :, :].rearrange("(k p) n -> p k n", p=128),
                              writes=[R_w])
                        wout_loaded[0] = True
                    pend = None
                    cc = 0
                    for ti in range(seq.ntiles):
                        m1_u(ti)
                        for q in range(nq):
                            c = cc % 2
                            cc += 1
                            m1_proj(ti, q, c)
                            if "nolag" in SKIP:
                                m1_sgu(ti, q, c)
                            else:
                                if pend is not None:
                                    m1_sgu(*pend)
                                pend = (ti, q, c)
                            if q == 0 and ti + 1 < seq.ntiles:
                                m1_load_norm(ti + 1)
                    if pend is not None:
                        m1_sgu(*pend)
                P.barrier()
                NB = nt
                with ExitStack() as es3:
                    xt = [sb(es3, nc, "e_xt%d" % i, [128, KC, 512], F32) for i in range(2)]
                    R_x = [mkres(KC) for _ in range(2)]
                    ch_x = [P.chan(), P.chan()]
                    ch_st = [P.chan("pool"), P.chan("pool")]
                    cn = [sb(es3, nc, "e_cn%d" % i, [128, ntk, NB], BF16) for i in range(2)]
                    R_cn = mkres(2)
                    ch_cn = [P.chan(), P.chan()]
                    fT = sb(es3, nc, "e_fT", [128, 4, NB], BF16)
                    R_f = mkres(4)
                    d_cs = g.d_dftx if seq.s == 0 else g.d_dftctx
                    nblk = T // NB
                    sym = (seq.s == 0) and (nblk >= 2) and (nblk % 2 == 0) and ("nosym" not in SKIP)
                    if not sym:
                        y12 = [sb(es3, nc, "e_y%d" % i, [128, NB], BF16) for i in range(2)]
                        R_y = mkres(2)

                    def load_cs(b, cs, slot):
                        P.dma("sp", ch_cn[slot], cn[slot][:, :, :],
                              d_cs[cs, :, b * NB:(b + 1) * NB].rearrange("(t p) n -> p t n", p=128), writes=[R_cn[slot]])
                    load_cs(0, 0, 0)
                    load_cs(0, 1, 1)
                    if sym:
                        nd = nblk // 2
                        fTm = [sb(es3, nc, "e_fTm%d" % i, [128, 4, NB], BF16) for i in range(2)]
                        R_fm = [mkres(4) for _ in range(2)]
                        ynq = sb(es3, nc, "e_ynq", [128, 4], BF16)
                        R_ynq = Res()
                        ocnt = [0]

                        def out_proj_tile(tile, fbuf, R_fb):
                            sx = ocnt[0] % 2
                            ocnt[0] += 1
                            P.dma("sp", ch_x[sx], xt[sx][:, :, :NB], seq.tile_ap(tile), reads=[seq.res[tile]], writes=R_x[sx])
                            for dch in range(KC):
                                ob = 4 + dch % 2
                                for k in range(KC):
                                    if k < 4:
                                        P.op("pe", lambda e, k=k: e.matmul(
                                            g.ps[ob][:, :NB], lhsT=wout[:, k, dch * 128:(dch + 1) * 128], rhs=fbuf[:, k, :],
                                            start=(k == 0), stop=False), reads=[R_w, R_fb[k]], writes=[g.Rps[ob]])
                                    else:
                                        P.op("pe", lambda e, k=k: e.matmul(
                                            g.ps[ob][:, :NB], lhsT=wout[:, k, dch * 128:(dch + 1) * 128],
                                            rhs=sgu[:, k - 4, tile * NB:(tile + 1) * NB], start=False, stop=(k == KC - 1)),
                                            reads=[R_w, R_sgu[tile]], writes=[g.Rps[ob]])
                                P.op("dve", lambda e: e.scalar_tensor_tensor(
                                    out=xt[sx][:, dch, :NB], in0=g.ps[ob][:, :NB], scalar=g.GT[:, l, 1, dch, seq.s:seq.s + 1],
                                    in1=xt[sx][:, dch, :NB], op0=ALU.mult, op1=ALU.add),
                                    reads=[g.Rps[ob], R_x[sx][dch], g.R_mods], writes=[R_x[sx][dch]])
                            P.dma("pool", ch_st[sx], seq.tile_ap(tile), xt[sx][:, :, :NB], reads=R_x[sx], writes=[seq.res[tile]])

                        y1s = [sb(es3, nc, "e_y1_%d" % i, [128, NB], BF16) for i in range(4)]
                        R_y1 = mkres(4)
                        y2s = [sb(es3, nc, "e_y2_%d" % i, [128, NB], BF16) for i in range(2)]
                        R_y2 = mkres(2)
                        for b in range(nd):
                            ms = b % 2
                            for h in range(4):
                                for t in range(ntk):
                                    P.op("pe", lambda e, t=t: e.matmul(
                                        g.ps[1][:, :NB], lhsT=a_tok[:, t, h * 128:(h + 1) * 128], rhs=cn[0][:, t, :],
                                        start=(t == 0), stop=(t == ntk - 1)),
                                        reads=[R_atok[t], R_cn[0]], writes=[g.Rps[1]])
                                P.op("act", lambda e: e.copy(out=y1s[h][:, :], in_=g.ps[1][:, :NB]),
                                     reads=[g.Rps[1]], writes=[R_y1[h]])
                            if b + 1 < nd:
                                load_cs(b + 1, 0, 0)
                            for h in range(4):
                                ys = h % 2
                                for t in range(ntk):
                                    P.op("pe", lambda e, t=t: e.matmul(
                                        g.ps[2][:, :NB], lhsT=a_tok[:, t, h * 128:(h + 1) * 128], rhs=cn[1][:, t, :],
                                        start=(t == 0), stop=(t == ntk - 1)),
                                        reads=[R_atok[t], R_cn[1]], writes=[g.Rps[2]])
                                P.op("dve", lambda e: e.tensor_copy(out=y2s[ys][:, :], in_=g.ps[2][:, :NB]),
                                     reads=[g.Rps[2]], writes=[R_y2[ys]])
                                if h == 3 and b + 1 < nd:
                                    load_cs(b + 1, 1, 1)
                                P.op("pe", lambda e: e.matmul(g.ps[3][:, :NB], lhsT=g.dftc[:, 0, :], rhs=y1s[h][:, :],
                                                              start=True, stop=False), reads=[R_y1[h]], writes=[g.Rps[3]])
                                P.op("pe", lambda e: e.matmul(g.ps[3][:, :NB], lhsT=g.dftc[:, 1, :], rhs=y2s[ys][:, :],
                                                              start=False, stop=True), reads=[R_y2[ys]], writes=[g.Rps[3]])
                                P.op("act", lambda e: e.copy(out=fT[:, h, :], in_=g.ps[3][:, :NB]),
                                     reads=[g.Rps[3]], writes=[R_f[h]])
                                P.op("pe", lambda e: e.matmul(g.ps[6][:, :NB], lhsT=g.dftc[:, 0, :], rhs=y1s[h][:, :],
                                                              start=True, stop=False), reads=[R_y1[h]], writes=[g.Rps[6]])
                                P.op("pe", lambda e: e.matmul(g.ps[6][:, :NB], lhsT=g.dftc[:, 2, :], rhs=y2s[ys][:, :],
                                                              start=False, stop=True), reads=[R_y2[ys]], writes=[g.Rps[6]])
                                P.op("dve", lambda e: e.tensor_copy(out=fTm[ms][:, h, NB - 1:0:-1], in_=g.ps[6][:, 1:NB]),
                                     reads=[g.Rps[6]], writes=[R_fm[ms][h]])
                                if b >= 1:
                                    P.op("dve", lambda e: e.tensor_copy(out=fTm[1 - ms][:, h, 0:1], in_=g.ps[6][:, 0:1]),
                                         reads=[g.Rps[6]], writes=[R_fm[1 - ms][h]])
                            out_proj_tile(b, fT, R_f)
                            if b >= 1:
                                out_proj_tile(nblk - b, fTm[1 - ms], R_fm[1 - ms])
                        msl = (nd - 1) % 2
                        for h in range(4):
                            for t in range(ntk):
                                P.op("pe", lambda e, t=t: e.matmul(
                                    g.ps[7][:, h:h + 1], lhsT=a_tok[:, t, h * 128:(h + 1) * 128], rhs=g.nyq[:, t:t + 1],
                                    start=(t == 0), stop=(t == ntk - 1)), reads=[R_atok[t]], writes=[g.Rps[7]])
                        P.op("act", lambda e: e.copy(out=ynq[:, :], in_=g.ps[7][:, 0:4]), reads=[g.Rps[7]], writes=[R_ynq])
                        for h in range(4):
                            P.op("pe", lambda e: e.matmul(g.ps[7][:, 8 + h:9 + h], lhsT=g.dftc[:, 0, :], rhs=ynq[:, h:h + 1],
                                                          start=True, stop=True), reads=[R_ynq], writes=[g.Rps[7]])
                        for h in range(4):
                            P.op("dve", lambda e: e.tensor_copy(out=fTm[msl][:, h, 0:1], in_=g.ps[7][:, 8 + h:9 + h]),
                                 reads=[g.Rps[7]], writes=[R_fm[msl][h]])
                        out_proj_tile(nblk - nd, fTm[msl], R_fm[msl])
                    for b in (range(nblk) if not sym else []):
                        sx = b % 2
                        P.dma("sp", ch_x[sx], xt[sx][:, :, :NB], seq.tile_ap(b), reads=[seq.res[b]], writes=R_x[sx])
                        for h in range(4):
                            for cs in range(2):
                                bank = 1 + cs
                                for t in range(ntk):
                                    P.op("pe", lambda e, t=t, h=h, cs=cs, bank=bank: e.matmul(
                                        g.ps[bank][:, :NB], lhsT=a_tok[:, t, h * 128:(h + 1) * 128], rhs=cn[cs][:, t, :],
                                        start=(t == 0), stop=(t == ntk - 1)),
                                        reads=[R_atok[t], R_cn[cs]], writes=[g.Rps[bank]])
                                if cs == 0:
                                    P.op("act", lambda e, bank=bank: e.copy(out=y12[0][:, :], in_=g.ps[bank][:, :NB]),
                                         reads=[g.Rps[bank]], writes=[R_y[0]])
                                else:
                                    P.op("dve", lambda e, bank=bank: e.tensor_copy(out=y12[1][:, :], in_=g.ps[bank][:, :NB]),
                                         reads=[g.Rps[bank]], writes=[R_y[1]])
                                if h == 3 and b + 1 < nblk:
                                    load_cs(b + 1, cs, cs)
                            P.op("pe", lambda e: e.matmul(g.ps[3][:, :NB], lhsT=g.dftc[:, 0, :], rhs=y12[0][:, :],
                                                          start=True, stop=False), reads=[R_y[0]], writes=[g.Rps[3]])
                            P.op("pe", lambda e: e.matmul(g.ps[3][:, :NB], lhsT=g.dftc[:, 1, :], rhs=y12[1][:, :],
                                                          start=False, stop=True), reads=[R_y[1]], writes=[g.Rps[3]])
                            P.op("act", lambda e, h=h: e.copy(out=fT[:, h, :], in_=g.ps[3][:, :NB]),
                                 reads=[g.Rps[3]], writes=[R_f[h]])
                        for dch in range(KC):
                            ob = 4 + dch % 2
                            for k in range(KC):
                                if k < 4:
                                    P.op("pe", lambda e, k=k, dch=dch, ob=ob: e.matmul(
                                        g.ps[ob][:, :NB], lhsT=wout[:, k, dch * 128:(dch + 1) * 128], rhs=fT[:, k, :],
                                        start=(k == 0), stop=False), reads=[R_w, R_f[k]], writes=[g.Rps[ob]])
                                else:
                                    P.op("pe", lambda e, k=k, dch=dch, ob=ob, b=b: e.matmul(
                                        g.ps[ob][:, :NB], lhsT=wout[:, k, dch * 128:(dch + 1) * 128],
                                        rhs=sgu[:, k - 4, b * NB:(b + 1) * NB], start=False, stop=(k == KC - 1)),
                                        reads=[R_w, R_sgu[b]], writes=[g.Rps[ob]])
                            P.op("dve", lambda e, ob=ob, dch=dch, sx=sx: e.scalar_tensor_tensor(
                                out=xt[sx][:, dch, :NB], in0=g.ps[ob][:, :NB], scalar=g.GT[:, l, 1, dch, seq.s:seq.s + 1],
                                in1=xt[sx][:, dch, :NB], op0=ALU.mult, op1=ALU.add),
                                reads=[g.Rps[ob], R_x[sx][dch], g.R_mods], writes=[R_x[sx][dch]])
                        P.dma("pool", ch_st[sx], seq.tile_ap(b), xt[sx][:, :, :NB], reads=R_x[sx], writes=[seq.res[b]])
                P.barrier()


def odd_mixer(P, g, l, seqx, seqc, last):
    nc = P.nc
    j = l // 2
    T = seqx.T
    TK = T + TC
    nkt = TK // 128
    ntx = seqx.ntiles
    with ExitStack() as es:
        kT = sb(es, nc, "o_kT", [128, NKV, TK], BF16)
        vtok = sb(es, nc, "o_v", [128, nkt, NKV * HD], BF16)
        qT = sb(es, nc, "o_qT", [128, NH, TK], BF16)
        R_k = [mkres(ntx + 1) for _ in range(NKV)]
        R_v = mkres(nkt)
        R_q = [mkres(ntx + 1) for _ in range(NH)]
        with ExitStack() as es2:
            wqkv = sb(es2, nc, "o_wqkv", [128, KC, 1536], BF16)
            R_wq = Res()
            chw = P.chan("pool")
            P.dma("pool", chw, wqkv[:, :, :], g.d_wqkv[j, :, :].rearrange("(k p) n -> p k n", p=128), writes=[R_wq])
            xt = sb(es2, nc, "o_xt", [128, KC, 512], F32)
            R_x = mkres(KC)
            ch_x = P.chan()
            hT = sb(es2, nc, "o_hT", [128, KC, 512], BF16)
            R_h = mkres(KC)
            W = alloc_norm_work(es2, nc, 0)
            cs_t = [sb(es2, nc, "o_cs%d" % i, [128, 2, 512], F32) for i in range(2)]
            R_cs = mkres(2)
            ch_cs = [P.chan(), P.chan()]
            t1f = [sb(es2, nc, "o_t1f%d" % i, [128, 512], F32) for i in range(2)]
            t1b = [sb(es2, nc, "o_t1b%d" % i, [128, 512], BF16) for i in range(2)]
            hsq = [sb(es2, nc, "o_hsq%d" % i, [128, 512], BF16) for i in range(2)]
            hsd = [sb(es2, nc, "o_hsd%d" % i, [128, 512], F32) for i in range(2)]
            hrs = [sb(es2, nc, "o_hrs%d" % i, [128, 512], F32) for i in range(2)]
            ta = [sb(es2, nc, "o_ta%d" % i, [128, 512], F32) for i in range(2)]
            tb = [sb(es2, nc, "o_tb%d" % i, [128, 512], F32) for i in range(2)]
            R_t1f, R_t1b, R_hsq, R_hsd, R_hrs, R_ta, R_tb = (mkres(2) for _ in range(7))
            state = dict(hc=0)

            def head_proj(wc0, nt):
                hs = state["hc"] % 2
                state["hc"] += 1
                bank = 2 + hs
                for k in range(KC):
                    P.op("pe", lambda e, k=k: e.matmul(
                        g.ps[bank][:, :nt], lhsT=wqkv[:, k, wc0:wc0 + 128], rhs=hT[:, k, :nt],
                        start=(k == 0), stop=(k == KC - 1)), reads=[R_wq, R_h[k]], writes=[g.Rps[bank]])
                return hs

            def head_finish(gcol, hs, nt, rope, sx, dst_ap, Rdst):
                bank = 2 + hs
                P.op("act", lambda e: e.activation(out=hsq[hs][:, :nt], in_=g.ps[bank][:, :nt], func=AF.Square),
                     reads=[g.Rps[bank]], writes=[R_hsq[hs]])
                P.op("act", lambda e: e.activation(out=t1f[hs][:, :nt], in_=g.ps[bank][:, :nt], func=AF.Identity,
                                                   scale=g.gqk[:, gcol:gcol + 1]),
                     reads=[g.Rps[bank], g.R_mods], writes=[R_t1f[hs]])
                if rope:
                    P.op("act", lambda e: e.activation(out=t1b[hs][:, :nt], in_=g.ps[bank][:, :nt], func=AF.Identity,
                                                       scale=g.gqk[:, gcol:gcol + 1]),
                         reads=[g.Rps[bank], g.R_mods], writes=[R_t1b[hs]])
                sbank = 4 + hs
                P.op("pe", lambda e: e.matmul(g.ps[sbank][:, :nt], lhsT=g.ones_hd[:, :], rhs=hsq[hs][:, :nt],
                                              start=True, stop=True), reads=[R_hsq[hs]], writes=[g.Rps[sbank]])
                rbank = 6 + hs
                if rope:
                    P.op("pe", lambda e: e.matmul(g.ps[rbank][:, :nt], lhsT=g.rotm[:, :], rhs=t1b[hs][:, :nt],
                                                  start=True, stop=True), reads=[R_t1b[hs]], writes=[g.Rps[rbank]])
                P.op("act", lambda e: e.activation(out=hsd[hs][:, :nt], in_=g.ps[sbank][:, :nt], func=AF.Ln,
                                                   bias=g.eps_t[:, 0:1], scale=1.0),
                     reads=[g.Rps[sbank]], writes=[R_hsd[hs]])
                P.op("act", lambda e: e.activation(out=hrs[hs][:, :nt], in_=hsd[hs][:, :nt], func=AF.Exp, scale=-0.5),
                     reads=[R_hsd[hs]], writes=[R_hrs[hs]])
                if rope:
                    P.op("dve", lambda e: e.tensor_tensor(out=ta[hs][:, :nt], in0=t1f[hs][:, :nt],
                                                          in1=cs_t[sx][:, 0, :nt], op=ALU.mult),
                         reads=[R_t1f[hs], R_cs[sx]], writes=[R_ta[hs]])
                    P.op("dve", lambda e: e.tensor_tensor(out=tb[hs][:, :nt], in0=g.ps[rbank][:, :nt],
                                                          in1=cs_t[sx][:, 1, :nt], op=ALU.mult),
                         reads=[g.Rps[rbank], R_cs[sx]], writes=[R_tb[hs]])
                    P.op("dve", lambda e: e.tensor_tensor(out=ta[hs][:, :nt], in0=ta[hs][:, :nt], in1=tb[hs][:, :nt],
                                                          op=ALU.add), reads=[R_ta[hs], R_tb[hs]], writes=[R_ta[hs]])
                    P.op("dve", lambda e: e.tensor_tensor(out=dst_ap, in0=ta[hs][:, :nt], in1=hrs[hs][:, :nt],
                                                          op=ALU.mult), reads=[R_ta[hs], R_hrs[hs]], writes=[Rdst])
                else:
                    P.op("dve", lambda e: e.tensor_tensor(out=dst_ap, in0=t1f[hs][:, :nt], in1=hrs[hs][:, :nt],
                                                          op=ALU.mult), reads=[R_t1f[hs], R_hrs[hs]], writes=[Rdst])

            tiles = [(seqx, ti) for ti in range(ntx)] + [(seqc, 0)]
            for idx, (seq, ti) in enumerate(tiles):
                nt = seq.nt
                sx = idx % 2
                col0 = ti * nt if seq.s == 0 else T
                rope = seq.s == 0
                P.dma("sp", ch_x, xt[:, :, :nt], seq.tile_ap(ti), reads=[seq.res[ti]], writes=R_x)
                if rope:
                    P.dma("sp", ch_cs[sx], cs_t[sx][:, :, :nt],
                          g.d_rope[:, :, col0:col0 + nt].rearrange("a p n -> p a n"), writes=[R_cs[sx]])
                emit_modnorm(P, g, l, 1, seq.s, xt, R_x, nt, hT, 0, R_h, W)
                for q in range(nt // 128):
                    tk = col0 // 128 + q
                    for k in range(KC):
                        P.op("pe", lambda e, k=k, q=q: e.matmul(
                            g.ps[1][:, 0:256], lhsT=hT[:, k, q * 128:(q + 1) * 128], rhs=wqkv[:, k, 1280:1536],
                            start=(k == 0), stop=(k == KC - 1)), reads=[R_wq, R_h[k]], writes=[g.Rps[1]])
                    P.op("act", lambda e, tk=tk: e.copy(out=vtok[:, tk, :], in_=g.ps[1][:, 0:256]),
                         reads=[g.Rps[1]], writes=[R_v[tk]])
                hl = [(1024 + h * 128, 2 + j, kT[:, h, col0:col0 + nt], R_k[h][idx]) for h in range(NKV)]
                if not (last and seq.s == 1):
                    hl += [(h * 128, 4 + j, qT[:, h, col0:col0 + nt], R_q[h][idx]) for h in range(NH)]
                hs_next = head_proj(hl[0][0], nt)
                for hi, (wc0, gcol, dst_ap, Rdst) in enumerate(hl):
                    hs = hs_next
                    if hi + 1 < len(hl):
                        hs_next = head_proj(hl[hi + 1][0], nt)
                    head_finish(gcol, hs, nt, rope, sx, dst_ap, Rdst)
        P.barrier()
        with ExitStack() as es2:
            wo = sb(es2, nc, "o_wo", [128, KC, D], BF16)
            R_wo = Res()
            chw_o = P.chan("pool")
            P.dma("pool", chw_o, wo[:, :, :], g.d_wo[j, :, :].rearrange("(k p) n -> p k n", p=128), writes=[R_wo])
            xt = [sb(es2, nc, "o_xt%d" % i, [128, KC, 512], F32) for i in range(2)]
            R_x = [mkres(KC) for _ in range(2)]
            ch_x = [P.chan(), P.chan()]
            ch_st = [P.chan(), P.chan()]
            pT = [sb(es2, nc, "o_pT%d" % i, [128, 2, 512], BF16) for i in range(3)]
            R_p = [mkres(2) for _ in range(3)]
            acc = [sb(es2, nc, "o_acc%d" % i, [128, 2, 512], F32) for i in range(2)]
            R_acc = mkres(2)
            oT = [sb(es2, nc, "o_oT%d" % i, [128, NH, 512], BF16) for i in range(2)]
            R_o = [mkres(NH) for _ in range(2)]
            lden = sb(es2, nc, "o_lden", [128, 512], F32)
            rden = sb(es2, nc, "o_rden", [128, 512], F32)
            R_lden, R_rden = Res(), Res()
            state = dict(pc=0, hcnt=0)
            pend_head = []
            pend_proj = []
            qtiles = [(seqx, ti) for ti in range(ntx)] + ([] if last else [(seqc, 0)])
            if "oddA2" in SKIP:
                qtiles = []

            def load_x(idx):
                seq, ti = qtiles[idx]
                sx = idx % 2
                P.dma("sp", ch_x[sx], xt[sx][:, :, :seq.nt], seq.tile_ap(ti), reads=[seq.res[ti]], writes=R_x[sx])

            def finish_stage(stage, h, a, nt, obank, dbank, so):
                if stage == 0:
                    P.op("dve", lambda e: e.tensor_tensor(out=acc[a][:, 0, :nt], in0=acc[a][:, 0, :nt],
                                                          in1=acc[a][:, 1, :nt], op=ALU.add),
                         reads=[R_acc[a]], writes=[R_acc[a]])
                elif stage == 1:
                    P.op("pe", lambda e: e.matmul(g.ps[dbank][:, :nt], lhsT=g.ones_f[:, :], rhs=acc[a][:, 0, :nt],
                                                  start=True, stop=True), reads=[R_acc[a]], writes=[g.Rps[dbank]])
                elif stage == 2:
                    P.op("act", lambda e: e.activation(out=lden[:, :nt], in_=g.ps[dbank][:, :nt], func=AF.Ln),
                         reads=[g.Rps[dbank]], writes=[R_lden])
                    P.op("act", lambda e: e.activation(out=rden[:, :nt], in_=lden[:, :nt], func=AF.Exp, scale=-1.0),
                         reads=[R_lden], writes=[R_rden])
                else:
                    P.op("dve", lambda e: e.tensor_tensor(out=oT[so][:, h, :nt], in0=g.ps[obank][:, :nt],
                                                          in1=rden[:, :nt], op=ALU.mult),
                         reads=[g.Rps[obank], R_rden], writes=[R_o[so][h]])

            STAGE_AT = (5, 8, 10, 13)

            def advance_heads(ki):
                for p in list(pend_head):
                    while p["stage"] < 4 and (ki is None or ki >= STAGE_AT[p["stage"]]):
                        finish_stage(p["stage"], *p["args"])
                        p["stage"] += 1
                    if p["stage"] == 4:
                        pend_head.remove(p)

            def out_proj_chunk(idx, dch):
                seq, ti = qtiles[idx]
                nt = seq.nt
                sx = idx % 2
                so = idx % 2
                ob = 0 if dch % 2 == 0 else 7
                for k in range(KC):
                    P.op("pe", lambda e, k=k: e.matmul(
                        g.ps[ob][:, :nt], lhsT=wo[:, k, dch * 128:(dch + 1) * 128], rhs=oT[so][:, k, :nt],
                        start=(k == 0), stop=(k == KC - 1)), reads=[R_wo, R_o[so][k]], writes=[g.Rps[ob]])
                P.op("dve", lambda e: e.scalar_tensor_tensor(
                    out=xt[sx][:, dch, :nt], in0=g.ps[ob][:, :nt], scalar=g.GT[:, l, 1, dch, seq.s:seq.s + 1],
                    in1=xt[sx][:, dch, :nt], op0=ALU.mult, op1=ALU.add),
                    reads=[g.Rps[ob], R_x[sx][dch], g.R_mods], writes=[R_x[sx][dch]])
                if dch == KC - 1:
                    P.dma("sp", ch_st[sx], seq.tile_ap(ti), xt[sx][:, :, :nt], reads=R_x[sx], writes=[seq.res[ti]])

            def proj_step(n):
                done = False
                while n > 0 and pend_proj:
                    pidx, dch = pend_proj[0]
                    out_proj_chunk(pidx, dch)
                    if dch == KC - 1:
                        pend_proj.pop(0)
                        done = True
                    else:
                        pend_proj[0] = (pidx, dch + 1)
                    n -= 1
                return done

            if qtiles:
                load_x(0)
            for idx, (seq, ti) in enumerate(qtiles):
                nt = seq.nt
                so = idx % 2
                col0 = ti * nt if seq.s == 0 else T
                qidx = ti if seq.s == 0 else ntx
                next_loaded = False
                ktiles = list(range(nkt)) if seq.s == 0 else list(range(T // 128, nkt))
                nk = len(ktiles)
                for h in range(NH):
                    kvh = h // (NH // NKV)
                    hc = state["hcnt"]
                    state["hcnt"] += 1
                    obank, dbank = 4 + hc % 2, 6
                    a = hc % 2
                    base = state["pc"]

                    def emit_scores(ki):
                        kt = ktiles[ki]
                        gk = base + ki
                        sb_ = 1 + gk % 3
                        kidx = (kt * 128 // seqx.nt) if kt * 128 < T else ntx
                        P.op("pe", lambda e: e.matmul(
                            g.ps[sb_][:, :nt], lhsT=kT[:, kvh, kt * 128:(kt + 1) * 128], rhs=qT[:, h, col0:col0 + nt],
                            start=True, stop=True), reads=[R_k[kvh][kidx], R_q[h][qidx]], writes=[g.Rps[sb_]])
                    emit_scores(0)
                    if nk > 1:
                        emit_scores(1)
                    for ki, kt in enumerate(ktiles):
                        gk = base + ki
                        sb_ = 1 + gk % 3
                        slot, half = (gk // 2) % 3, gk % 2
                        if ki + 2 < nk:
                            emit_scores(ki + 2)
                        P.op("act", lambda e: e.activation(out=pT[slot][:, half, :nt], in_=g.ps[sb_][:, :nt], func=AF.Exp),
                             reads=[g.Rps[sb_]], writes=[R_p[slot][half]])
                        P.op("pe", lambda e: e.matmul(
                            g.ps[obank][:, :nt], lhsT=vtok[:, kt, kvh * 128:(kvh + 1) * 128], rhs=pT[slot][:, half, :nt],
                            start=(ki == 0), stop=(ki == nk - 1)), reads=[R_v[kt], R_p[slot][half]], writes=[g.Rps[obank]])
                        if half == 1:
                            if ki == 1:
                                P.op("dve", lambda e: e.tensor_copy(out=acc[a][:, :, :nt], in_=pT[slot][:, :, :nt]),
                                     reads=R_p[slot], writes=[R_acc[a]])
                            else:
                                P.op("dve", lambda e: e.tensor_tensor(out=acc[a][:, :, :nt], in0=acc[a][:, :, :nt],
                                                                      in1=pT[slot][:, :, :nt], op=ALU.add),
                                     reads=R_p[slot] + [R_acc[a]], writes=[R_acc[a]])
                        advance_heads(ki)
                        if ki == 20 and pend_proj:
                            proj_step(1)
                        if not pend_proj and not next_loaded and idx + 1 < len(qtiles) and (h > 0 or ki >= 16):
                            load_x(idx + 1)
                            next_loaded = True
                    state["pc"] += nk
                    advance_heads(None)
                    pend_head.append(dict(stage=0, args=(h, a, nt, obank, dbank, so)))
                    if nk <= STAGE_AT[-1]:
                        advance_heads(None)
                proj_step(10 ** 6)
                if idx + 1 < len(qtiles) and not next_loaded:
                    load_x(idx + 1)
                pend_proj.append((idx, 0))
            advance_heads(None)
            proj_step(10 ** 6)
        P.barrier()


def build_program(T, L=4, debug_stream=False):
    nc = bass.Bass("TRN2", target_bir_lowering=False)
    g = G()
    g.L = L
    n_even = (L + 1) // 2
    n_odd = max(L // 2, 1)

    def din(name, shape, dt=F32):
        return nc.dram_tensor(name, shape, dt, kind="ExternalInput").ap()
    g.d_x = din("x", [T, D])
    g.d_ctx = din("ctx", [TC, D])
    g.d_c = din("c", [1, D])
    g.d_cctx = din("c_ctx", [1, D])
    g.d_wmod = din("w_mod", [L, D, NMOD * D])
    g.d_bmod = din("b_mod", [1, L, NMOD * D])
    g.d_gffn1 = din("g_ffn1", [1, L, D])
    g.d_gmix = din("g_mix", [1, L, D])
    g.d_gffn2 = din("g_ffn2", [1, L, D])
    g.d_gfinal = din("g_final", [1, D])
    g.d_wgu1 = din("w_ffn1_gu", [L, D, 2 * DFF])
    g.d_wd1 = din("w_ffn1_down", [L, DFF, D])
    g.d_wgu2 = din("w_ffn2_gu", [L, D, 2 * DFF])
    g.d_wd2 = din("w_ffn2_down", [L, DFF, D])
    g.d_win = din("w_in_ab", [n_even, D, 1536])
    g.d_gv = din("g_v", [n_even, 4, 128])
    g.d_ws = din("w_s", [n_even, 4, 128, 128])
    g.d_bs = din("b_s", [n_even, 4, 128])
    g.d_wout = din("w_out_ab", [n_even, D, D])
    g.d_wqkv = din("w_qkv", [n_odd, D, 1536])
    g.d_gq = din("g_q", [1, n_odd, HD])
    g.d_gk = din("g_k", [1, n_odd, HD])
    g.d_wo = din("w_o", [n_odd, D, D])
    g.d_ident = din("k_ident", [128, 128])
    g.d_rot = din("k_rot", [128, 128])
    g.d_dftc = din("k_dftc", [3, 128, 128])
    g.d_nyq = din("k_nyq", [128, T // 128])
    g.d_dftx = din("k_dftx", [2, T, T], BF16)
    g.d_dftctx = din("k_dftctx", [2, TC, TC], BF16)
    g.d_rope = din("k_rope", [2, 128, T])
    y = nc.dram_tensor("y", [T, D], F32, kind="ExternalOutput").ap()
    xT = nc.dram_tensor("xT_scr", [D, T], F32).ap()
    cT = nc.dram_tensor("cT_scr", [D, TC], F32).ap()
    if debug_stream:
        dbg_x = nc.dram_tensor("dbg_x", [D, T], F32, kind="ExternalOutput").ap()
        dbg_c = nc.dram_tensor("dbg_c", [D, TC], F32, kind="ExternalOutput").ap()

    with ExitStack() as es:
        P = Prog(nc, es)
        g.ps = [es.enter_context(nc.psum_tensor("ps%d" % i, [128, 512], F32)) for i in range(8)]
        g.Rps = mkres(8)
        g.R_out = Res()
        g.R_mods = Res()
        g.ones_dm = sb(es, nc, "ones_dm", [128, 128], BF16)
        g.ones_hd = sb(es, nc, "ones_hd", [128, 128], BF16)
        g.ones_1 = sb(es, nc, "ones_1", [128, 128], BF16)
        g.one11 = sb(es, nc, "one11", [1, 2], F32)
        g.ones_f = sb(es, nc, "ones_f", [128, 128], F32)
        g.eps_t = sb(es, nc, "eps_t", [128, 1], F32)
        g.ident = sb(es, nc, "ident", [128, 128], F32)
        g.rotm_f = sb(es, nc, "rotm_f", [128, 128], F32)
        g.rotm = sb(es, nc, "rotm", [128, 128], BF16)
        g.dftc_f = sb(es, nc, "dftc_f", [128, 3, 128], F32)
        g.nyq_f = sb(es, nc, "nyq_f", [128, T // 128], F32)
        g.nyq = sb(es, nc, "nyq", [128, T // 128], BF16)
        g.dftc = sb(es, nc, "dftc", [128, 3, 128], BF16)
        g.mods = sb(es, nc, "mods", [128, L, NMOD * KC, 2], F32)
        g.gT = sb(es, nc, "gT", [128, 3 * L + 1, KC], F32)
        g.gqk = sb(es, nc, "gqk", [128, 8], F32)
        g.GS = sb(es, nc, "GS", [128, L, 3, KC, 2], F32)
        g.GT = sb(es, nc, "GT", [128, L, 3, KC, 2], F32)
        P.op("dve", lambda e: e.memset(g.eps_t[:, :], EPS), writes=[g.R_mods])

        seqx = Seq(xT, T, 0, 512)
        seqc = Seq(cT, TC, 1, 256)
        prologue(P, g)
        to_feature_major(P, g, g.d_x, seqx)
        to_feature_major(P, g, g.d_ctx, seqc)
        ffn_phase(P, g, [(0, 0, g.d_wgu1, g.d_wd1, make_supers(seqx, seqc))])
        for l in range(L):
            last = l == L - 1
            even = l % 2 == 0
            if even:
                if "even" not in SKIP:
                    even_mixer(P, g, l, [seqx] if (last or "evenctx" in SKIP) else [seqx, seqc])
            else:
                if "odd" not in SKIP:
                    odd_mixer(P, g, l, seqx, seqc, last)
            jobs = [(l, 2, g.d_wgu2, g.d_wd2, make_supers(seqx, None if last else seqc))]
            if not last:
                jobs.append((l + 1, 0, g.d_wgu1, g.d_wd1, make_supers(seqx, seqc)))
            ffn_phase(P, g, jobs)
        if debug_stream:
            chd = P.chan()
            P.dma("sp", chd, dbg_x[:, :], xT[:, :], reads=seqx.res, writes=[g.R_out])
            P.dma("sp", chd, dbg_c[:, :], cT[:, :], reads=seqc.res, writes=[g.R_out])
        final_phase(P, g, seqx, y)
    return nc


def host_constants(T):
    bf = ml_dtypes.bfloat16
    k = {}
    k["k_ident"] = np.eye(128, dtype=np.float32)
    rot = np.zeros((128, 128), np.float32)
    for base in (0, 64):
        for jj in range(32):
            rot[base + jj + 32, base + jj] = -1.0
            rot[base + jj, base + jj + 32] = 1.0
    k["k_rot"] = rot

    def dft(n, scale):
        idx = np.arange(n, dtype=np.int64)
        m = (idx[:, None] * idx[None, :]) % n
        ang = 2.0 * np.pi * m.astype(np.float64) / n
        return np.cos(ang) * scale, np.sin(ang) * scale
    cc, sc_ = dft(128, 128 ** -0.5)
    k["k_dftc"] = np.stack([cc, -sc_, sc_]).astype(np.float32)
    pp = np.arange(128)
    k["k_nyq"] = np.repeat((((-1.0) ** pp) * T ** -0.5)[:, None], T // 128, axis=1).astype(np.float32)
    cx, sx = dft(T, T ** -0.5)
    k["k_dftx"] = np.stack([cx.astype(np.float32), sx.astype(np.float32)]).astype(bf)
    del cx, sx
    c2, s2 = dft(TC, TC ** -0.5)
    k["k_dftctx"] = np.stack([c2.astype(np.float32), s2.astype(np.float32)]).astype(bf)
    rows = T // GRID_W
    row = np.repeat(np.arange(rows, dtype=np.float32), GRID_W)
    col = np.tile(np.arange(GRID_W, dtype=np.float32), rows)
    inv = (np.float32(10000.0) ** (-np.arange(0, 64, 2, dtype=np.float32) / np.float32(64))).astype(np.float32)
    ar = row[:, None] * inv[None, :]
    ac = col[:, None] * inv[None, :]
    full = np.concatenate([ar, ar, ac, ac], axis=1)
    k["k_rope"] = np.stack([np.cos(full).T, np.sin(full).T]).astype(np.float32)
    return k


_CACHE = {}


def make_in_maps(inputs, T, L, B):
    f = lambda a: np.ascontiguousarray(np.asarray(a, dtype=np.float32))
    shared = {
        "c_ctx": f(inputs["c_ctx"]).reshape(1, D),
        "w_mod": f(inputs["w_mod"])[:L],
        "b_mod": f(inputs["b_mod"])[:L].reshape(1, L, NMOD * D),
        "g_ffn1": f(inputs["g_ffn1"])[:L].reshape(1, L, D),
        "g_mix": f(inputs["g_mix"])[:L].reshape(1, L, D),
        "g_ffn2": f(inputs["g_ffn2"])[:L].reshape(1, L, D),
        "g_final": f(inputs["g_final"]).reshape(1, D),
        "w_ffn1_gu": f(inputs["w_ffn1_gu"])[:L],
        "w_ffn1_down": f(inputs["w_ffn1_down"])[:L],
        "w_ffn2_gu": f(inputs["w_ffn2_gu"])[:L],
        "w_ffn2_down": f(inputs["w_ffn2_down"])[:L],
        "w_in_ab": f(inputs["w_in_ab"]),
        "g_v": f(inputs["g_v"]),
        "w_s": f(inputs["w_s"]),
        "b_s": f(inputs["b_s"]),
        "w_out_ab": f(inputs["w_out_ab"]),
        "w_qkv": f(inputs["w_qkv"]),
        "g_q": f(inputs["g_q"]).reshape(1, -1, HD),
        "g_k": f(inputs["g_k"]).reshape(1, -1, HD),
        "w_o": f(inputs["w_o"]),
    }
    n_even = (L + 1) // 2
    n_odd = max(L // 2, 1)
    for nm in ("w_in_ab", "g_v", "w_s", "b_s", "w_out_ab"):
        shared[nm] = shared[nm][:n_even]
    for nm in ("w_qkv", "w_o"):
        shared[nm] = shared[nm][:n_odd]
    shared["g_q"] = shared["g_q"][:, :n_odd]
    shared["g_k"] = shared["g_k"][:, :n_odd]
    shared.update(host_constants(T))
    x = f(inputs["x"])
    ctx = f(inputs["ctx"])
    c = f(inputs["c"])
    maps = []
    for b in range(B):
        m = dict(shared)
        m["x"] = x[b]
        m["ctx"] = ctx[b]
        m["c"] = c[b].reshape(1, D)
        maps.append(m)
    return maps


def kernel(**inputs):
    x = np.asarray(inputs["x"])
    B, T, _ = x.shape
    L = 4
    key = (T, L)
    if key not in _CACHE:
        _CACHE[key] = build_program(T, L)
    nc = _CACHE[key]
    maps = make_in_maps(inputs, T, L, B)
    res = run_bass_kernel_spmd(nc, maps, core_ids=list(range(B)))
    out = np.stack([np.asarray(res.results[b]["y"], dtype=np.float32) for b in range(B)], axis=0)
    return out
```

```python
import numpy as np
from contextlib import ExitStack
import ml_dtypes
import concourse.bass as bass
import concourse.mybir as mybir
from concourse.bass_utils import run_bass_kernel_spmd

F32 = mybir.dt.float32
BF16 = mybir.dt.bfloat16
AF = mybir.ActivationFunctionType
ALU = mybir.AluOpType
AX = mybir.AxisListType

D = 1024
DFF = 2752
NMOD = 9
EPS = 1e-6
KC = 8
FCH = 22
GRID_W = 64
TC = 256
HD = 128
NH = 8
NKV = 2


class Res:
    __slots__ = ("w", "r")

    def __init__(self):
        self.w = None
        self.r = {}


def mkres(n):
    return [Res() for _ in range(n)]


class Chan:
    def __init__(self, sem):
        self.sem = sem
        self.cnt = 0


class Prog:
    def __init__(self, nc, es):
        self.nc = nc
        self.es = es
        self.eng = {}
        for name, e in (("pe", nc.tensor), ("act", nc.scalar), ("dve", nc.vector),
                        ("pool", nc.gpsimd), ("sp", nc.sync)):
            sem = es.enter_context(nc.semaphore("s_" + name))
            self.eng[name] = dict(e=e, sem=sem, cnt=0, seen={})
        self.pool_ch = []
        self.ch_kind = {}
        self.ch_kidx = {}

    def chan(self, kind="sp"):
        lst = self.ch_kind.setdefault(kind, [])
        idx = self.ch_kidx.get(kind, 0)
        if idx == len(lst):
            sem = self.es.enter_context(self.nc.semaphore("c%s%d" % (kind, idx)))
            c = Chan(sem)
            lst.append(c)
            self.pool_ch.append(c)
        self.ch_kidx[kind] = idx + 1
        return lst[idx]

    def _wait(self, en, ev):
        sem, val, owner = ev
        if owner == en and en == "pe":
            return
        E = self.eng[en]
        k = id(sem)
        if E["seen"].get(k, 0) >= val:
            return
        E["e"].wait_ge(sem, val)
        E["seen"][k] = val

    def _deps(self, en, reads, writes):
        evs = {}

        def add(ev):
            if ev is None:
                return
            k = id(ev[0])
            if k not in evs or evs[k][1] < ev[1]:
                evs[k] = ev
        for r in reads:
            add(r.w)
        for r in writes:
            add(r.w)
            for ev in r.r.values():
                add(ev)
        for ev in evs.values():
            self._wait(en, ev)

    @staticmethod
    def _commit(ev, reads, writes):
        k = id(ev[0])
        for r in reads:
            r.r[k] = ev
        for r in writes:
            r.w = ev
            r.r = {}

    def op(self, en, fn, reads=(), writes=()):
        self._deps(en, reads, writes)
        E = self.eng[en]
        ins = fn(E["e"])
        E["cnt"] += 1
        ins.then_inc(E["sem"], 1)
        ev = (E["sem"], E["cnt"], en)
        self._commit(ev, reads, writes)
        return ev

    def dma(self, en, ch, out, in_, reads=(), writes=()):
        self._deps(en, reads, writes)
        E = self.eng[en]
        ins = E["e"].dma_start(out=out, in_=in_)
        ch.cnt += 16
        ins.then_inc(ch.sem, 16)
        ev = (ch.sem, ch.cnt, None)
        self._commit(ev, reads, writes)
        return ev

    def barrier(self):
        sp = self.eng["sp"]
        for name, E in self.eng.items():
            if name != "sp" and E["cnt"] > 0:
                self._wait("sp", (E["sem"], E["cnt"], name))
        for c in self.pool_ch:
            if c.cnt:
                self._wait("sp", (c.sem, c.cnt, None))
        sp["cnt"] += 1
        sp["e"].sem_inc(sp["sem"], 1)
        ev = (sp["sem"], sp["cnt"], "sp")
        for name in self.eng:
            if name != "sp":
                self._wait(name, ev)
        for name, E in self.eng.items():
            for n2, E2 in self.eng.items():
                E["seen"][id(E2["sem"])] = E2["cnt"]
            for c in self.pool_ch:
                E["seen"][id(c.sem)] = c.cnt
        self.ch_kidx = {}


class G:
    pass


class Seq:
    def __init__(self, ap, T, s, nt):
        self.ap = ap
        self.T = T
        self.s = s
        self.nt = nt
        self.ntiles = T // nt
        self.res = mkres(self.ntiles)

    def tile_ap(self, i, c0=0, c1=KC):
        return self.ap[c0 * 128:c1 * 128, i * self.nt:(i + 1) * self.nt].rearrange(
            "(k p) t -> p k t", p=128)


_UID = [0]
SKIP = set()


def sb(es, nc, name, shape, dt):
    _UID[0] += 1
    return es.enter_context(nc.sbuf_tensor("%s_%d" % (name, _UID[0]), shape, dt))


def prologue(P, g):
    nc = P.nc
    L = g.L
    R_c = Res()
    P.op("dve", lambda e: e.memset(g.ones_dm[:, :], 1.0 / D), writes=[R_c])
    P.op("dve", lambda e: e.memset(g.ones_hd[:, :], 1.0 / HD), writes=[R_c])
    P.op("dve", lambda e: e.memset(g.ones_1[:, :], 1.0), writes=[R_c])
    P.op("dve", lambda e: e.memset(g.one11[:, :], 1.0), writes=[R_c])
    P.op("dve", lambda e: e.memset(g.ones_f[:, :], 1.0), writes=[R_c])
    ch = P.chan()
    P.dma("sp", ch, g.ident[:, :], g.d_ident[:, :], writes=[R_c])
    P.dma("sp", ch, g.rotm_f[:, :], g.d_rot[:, :], writes=[R_c])
    P.dma("sp", ch, g.dftc_f[:, :, :], g.d_dftc.rearrange("a p n -> p a n"), writes=[R_c])
    P.op("dve", lambda e: e.tensor_copy(out=g.rotm[:, :], in_=g.rotm_f[:, :]), reads=[R_c], writes=[R_c])
    P.op("dve", lambda e: e.tensor_copy(out=g.dftc[:, :, :], in_=g.dftc_f[:, :, :]), reads=[R_c], writes=[R_c])
    P.dma("sp", ch, g.nyq_f[:, :], g.d_nyq[:, :], writes=[R_c])
    P.op("dve", lambda e: e.tensor_copy(out=g.nyq[:, :], in_=g.nyq_f[:, :]), reads=[R_c], writes=[R_c])
    with ExitStack() as es:
        crow = sb(es, nc, "crow", [1, 2, D], F32)
        grow = sb(es, nc, "grow", [1, 3 * L + 1, D], F32)
        qkrow = sb(es, nc, "qkrow", [1, 4, HD], F32)
        brow = sb(es, nc, "brow", [1, NMOD * D], F32)
        R_brow = Res()
        ch_b = P.chan()
        sc = sb(es, nc, "sc", [128, KC, 2], F32)
        NP = 9
        PW = NMOD * D // NP
        wm = [sb(es, nc, "wm%d" % i, [128, KC, PW], BF16) for i in range(2)]
        scb = sb(es, nc, "scb", [128, KC, 2], BF16)
        R_rows = Res()
        ch2 = P.chan()
        P.dma("sp", ch2, crow[0:1, 0, :], g.d_c[0:1, :], writes=[R_rows])
        P.dma("sp", ch2, crow[0:1, 1, :], g.d_cctx[0:1, :], writes=[R_rows])
        for n, dt_ in enumerate((g.d_gffn1, g.d_gmix, g.d_gffn2)):
            P.dma("sp", ch2, grow[0:1, n * L:(n + 1) * L, :], dt_[0:1, :, :], writes=[R_rows])
        P.dma("sp", ch2, grow[0:1, 3 * L, :], g.d_gfinal[0:1, :], writes=[R_rows])
        n_odd = max(L // 2, 1)
        P.op("dve", lambda e: e.memset(qkrow[0:1, :, :], 1.0), writes=[R_rows])
        P.dma("sp", ch2, qkrow[0:1, 0:n_odd, :], g.d_gq[0:1, :, :], writes=[R_rows])
        P.dma("sp", ch2, qkrow[0:1, 2:2 + n_odd, :], g.d_gk[0:1, :, :], writes=[R_rows])
        ps = g.ps
        Rp = g.Rps
        for k in range(KC):
            for s in range(2):
                P.op("pe", lambda e, k=k, s=s: e.matmul(ps[0][:, 2 * k + s:2 * k + s + 1],
                                                       lhsT=crow[0:1, s, k * 128:(k + 1) * 128],
                                                       rhs=g.one11[0:1, 0:1], start=True, stop=True),
                     reads=[R_rows, R_c], writes=[Rp[0]])
        R_sc = Res()
        P.op("act", lambda e: e.activation(out=sc[:, :, :].rearrange("p k s -> p (k s)"), in_=ps[0][:, 0:16],
                                           func=AF.Silu), reads=[Rp[0]], writes=[R_sc])
        P.op("dve", lambda e: e.tensor_copy(out=scb[:, :, :], in_=sc[:, :, :]), reads=[R_sc], writes=[R_sc])
        ng = 3 * L + 1
        for n in range(ng):
            for k in range(KC):
                P.op("pe", lambda e, n=n, k=k: e.matmul(ps[1][:, n * KC + k:n * KC + k + 1],
                                                       lhsT=grow[0:1, n, k * 128:(k + 1) * 128],
                                                       rhs=g.one11[0:1, 0:1], start=True, stop=True),
                     reads=[R_rows, R_c], writes=[Rp[1]])
        for n in range(4):
            P.op("pe", lambda e, n=n: e.matmul(ps[1][:, ng * KC + n:ng * KC + n + 1],
                                               lhsT=qkrow[0:1, n, :], rhs=g.one11[0:1, 0:1],
                                               start=True, stop=True),
                 reads=[R_rows, R_c], writes=[Rp[1]])
        P.op("dve", lambda e: e.tensor_copy(out=g.gT[:, :, :].rearrange("p n k -> p (n k)"),
                                            in_=ps[1][:, 0:ng * KC]), reads=[Rp[1]], writes=[g.R_mods])
        P.op("dve", lambda e: e.tensor_copy(out=g.gqk[:, 0:4], in_=ps[1][:, ng * KC:ng * KC + 4]),
             reads=[Rp[1]], writes=[g.R_mods])
        P.op("dve", lambda e: e.tensor_scalar(out=g.gqk[:, 4:6], in0=g.gqk[:, 0:2], scalar1=float(HD) ** -0.5,
                                              scalar2=None, op0=ALU.mult), reads=[g.R_mods], writes=[g.R_mods])
        R_wm = mkres(2)
        chw = [P.chan("pool"), P.chan("pool")]
        pi = 0
        for l in range(L):
            P.dma("sp", ch_b, brow[0:1, :], g.d_bmod[0:1, l, :], writes=[R_brow])
            for pc in range(NP):
                slot = pi % 2
                P.dma("pool", chw[slot], wm[slot][:, :, :],
                      g.d_wmod[l, :, pc * PW:(pc + 1) * PW].rearrange("(k p) n -> p k n", p=128),
                      writes=[R_wm[slot]])
                bank = 2 + slot
                for cc in range(PW // 128):
                    gc = pc * (PW // 128) + cc
                    for k in range(KC):
                        P.op("pe", lambda e, k=k, cc=cc, slot=slot, bank=bank: e.matmul(
                            ps[bank][:, 2 * cc:2 * cc + 2], lhsT=wm[slot][:, k, cc * 128:(cc + 1) * 128],
                            rhs=scb[:, k, :], start=(k == 0), stop=False),
                            reads=[R_wm[slot], R_sc], writes=[Rp[bank]])
                    P.op("pe", lambda e, cc=cc, gc=gc, l=l, bank=bank: e.matmul(
                        ps[bank][:, 2 * cc:2 * cc + 2], lhsT=brow[0:1, gc * 128:(gc + 1) * 128],
                        rhs=g.one11[0:1, 0:2], start=False, stop=True),
                        reads=[R_brow, R_c], writes=[Rp[bank]])
                nch = PW // 128
                P.op("dve", lambda e, l=l, pc=pc, bank=bank, nch=nch: e.tensor_copy(
                    out=g.mods[:, l, pc * nch:(pc + 1) * nch, :].rearrange("p c s -> p (c s)"),
                    in_=ps[bank][:, 0:2 * nch]), reads=[Rp[bank]], writes=[g.R_mods])
                pi += 1
        for l in range(L):
            for n in range(3):
                sh_i, sc_i, gt_i = 3 * n, 3 * n + 1, 3 * n + 2
                P.op("dve", lambda e, l=l, n=n, sc_i=sc_i: e.tensor_scalar(
                    out=g.GS[:, l, n, :, :], in0=g.mods[:, l, sc_i * KC:(sc_i + 1) * KC, :], scalar1=1.0,
                    scalar2=None, op0=ALU.add), reads=[g.R_mods], writes=[g.R_mods])
                P.op("dve", lambda e, l=l, n=n: e.tensor_tensor(
                    out=g.GS[:, l, n, :, :], in0=g.GS[:, l, n, :, :],
                    in1=g.gT[:, n * L + l, :].unsqueeze(2).broadcast_to([128, KC, 2]), op=ALU.mult),
                    reads=[g.R_mods], writes=[g.R_mods])
                P.op("dve", lambda e, l=l, n=n, gt_i=gt_i: e.tensor_scalar(
                    out=g.GT[:, l, n, :, :], in0=g.mods[:, l, gt_i * KC:(gt_i + 1) * KC, :],
                    scalar1=(1.0 if n == 1 else 0.5), scalar2=None, op0=ALU.mult),
                    reads=[g.R_mods], writes=[g.R_mods])
        P.barrier()


def shift_ap(g, l, n, c, s):
    i = 3 * n
    return g.mods[:, l, i * KC + c, s:s + 1]


def to_feature_major(P, g, src, seq):
    nc = P.nc
    with ExitStack() as es:
        xin = [sb(es, nc, "xin%d" % i, [128, D], F32) for i in range(3)]
        xo = [sb(es, nc, "xo%d" % i, [128, KC, 512], F32) for i in range(2)]
        R_in = mkres(3)
        R_o = mkres(2)
        ch_in = [P.chan() for _ in range(3)]
        ch_o = [P.chan("pool") for _ in range(2)]
        nq = seq.nt // 128
        cnt = 0
        for i in range(seq.ntiles):
            so = i % 2
            for q in range(nq):
                si = cnt % 3
                cnt += 1
                r0 = i * seq.nt + q * 128
                P.dma("sp", ch_in[si], xin[si][:, :], src[r0:r0 + 128, :], writes=[R_in[si]])
                for c in range(KC):
                    P.op("pe", lambda e, c=c, q=q, si=si: e.transpose(
                        out=g.ps[c][:, q * 128:(q + 1) * 128], in_=xin[si][:, c * 128:(c + 1) * 128],
                        identity=g.ident[:, :]), reads=[R_in[si]], writes=[g.Rps[c]])
            for c in range(KC):
                en = "dve" if c % 2 == 0 else "act"
                if en == "dve":
                    P.op("dve", lambda e, c=c, so=so: e.tensor_copy(out=xo[so][:, c, :seq.nt], in_=g.ps[c][:, :seq.nt]),
                         reads=[g.Rps[c]], writes=[R_o[so]])
                else:
                    P.op("act", lambda e, c=c, so=so: e.copy(out=xo[so][:, c, :seq.nt], in_=g.ps[c][:, :seq.nt]),
                         reads=[g.Rps[c]], writes=[R_o[so]])
            P.dma("pool", ch_o[so], seq.tile_ap(i), xo[so][:, :, :seq.nt], reads=[R_o[so]], writes=[seq.res[i]])
        P.barrier()


def emit_norm_stats(P, g, xt, R_x, nt, sq, R_sq, bank, rstd, R_rstd, sd, R_sd):
    for c in range(KC):
        s = c % 2
        P.op("act", lambda e, c=c, s=s: e.activation(out=sq[s][:, :nt], in_=xt[:, c, :nt], func=AF.Square),
             reads=[R_x[c]], writes=[R_sq[s]])
        P.op("pe", lambda e, c=c, s=s: e.matmul(g.ps[bank][:, :nt], lhsT=g.ones_dm[:, :], rhs=sq[s][:, :nt],
                                                start=(c == 0), stop=(c == KC - 1)),
             reads=[R_sq[s]], writes=[g.Rps[bank]])
    P.op("act", lambda e: e.activation(out=sd[:, :nt], in_=g.ps[bank][:, :nt], func=AF.Ln, bias=g.eps_t[:, 0:1],
                                       scale=1.0), reads=[g.Rps[bank]], writes=[R_sd])
    P.op("act", lambda e: e.activation(out=rstd[:, :nt], in_=sd[:, :nt], func=AF.Exp, scale=-0.5),
         reads=[R_sd], writes=[R_rstd])


def emit_modnorm(P, g, l, n, s, xt, R_x, nt, hT, hoff, R_h, W, sq_in_h=False):
    if sq_in_h:
        bank = W["bank"]
        for c in range(KC):
            P.op("act", lambda e, c=c: e.activation(out=hT[:, c, hoff:hoff + nt], in_=xt[:, c, :nt], func=AF.Square),
                 reads=[R_x[c]], writes=[R_h[c]])
        for c in range(KC):
            P.op("pe", lambda e, c=c: e.matmul(g.ps[bank][:, :nt], lhsT=g.ones_dm[:, :], rhs=hT[:, c, hoff:hoff + nt],
                                               start=(c == 0), stop=(c == KC - 1)),
                 reads=[R_h[c]], writes=[g.Rps[bank]])
        P.op("act", lambda e: e.activation(out=W["sd"][:, :nt], in_=g.ps[bank][:, :nt], func=AF.Ln,
                                           bias=g.eps_t[:, 0:1], scale=1.0), reads=[g.Rps[bank]], writes=[W["R_sd"]])
        P.op("act", lambda e: e.activation(out=W["rstd"][:, :nt], in_=W["sd"][:, :nt], func=AF.Exp, scale=-0.5),
             reads=[W["R_sd"]], writes=[W["R_rstd"]])
    else:
        emit_norm_stats(P, g, xt, R_x, nt, W["sq"], W["R_sq"], W["bank"], W["rstd"], W["R_rstd"], W["sd"], W["R_sd"])
    for c in range(KC):
        ts = c % 2
        tb = W["t"][ts]
        P.op("dve", lambda e, c=c, tb=tb: e.scalar_tensor_tensor(
            out=tb[:, :nt], in0=xt[:, c, :nt], scalar=g.GS[:, l, n, c, s:s + 1], in1=W["rstd"][:, :nt],
            op0=ALU.mult, op1=ALU.mult), reads=[R_x[c], W["R_rstd"], g.R_mods], writes=[W["R_t"][ts]])
        P.op("act", lambda e, c=c, tb=tb: e.activation(
            out=hT[:, c, hoff:hoff + nt], in_=tb[:, :nt], func=AF.Identity, bias=shift_ap(g, l, n, c, s),
            scale=1.0), reads=[W["R_t"][ts], g.R_mods], writes=[R_h[c]])


def alloc_norm_work(es, nc, bank):
    W = {}
    W["sq"] = [sb(es, nc, "nsq%d" % i, [128, 512], BF16) for i in range(2)]
    W["R_sq"] = mkres(2)
    W["t"] = [sb(es, nc, "nt%d" % i, [128, 512], F32) for i in range(2)]
    W["R_t"] = mkres(2)
    W["rstd"] = sb(es, nc, "nrstd", [128, 512], F32)
    W["R_rstd"] = Res()
    W["sd"] = sb(es, nc, "nsd", [128, 512], F32)
    W["R_sd"] = Res()
    W["bank"] = bank
    return W


def final_phase(P, g, seq, dst):
    nc = P.nc
    L = g.L
    with ExitStack() as es:
        xt = [sb(es, nc, "fxt%d" % i, [128, KC, 512], F32) for i in range(2)]
        R_x = [mkres(KC) for _ in range(2)]
        yfs = [sb(es, nc, "fyf%d" % i, [128, KC, 512], F32) for i in range(2)]
        R_yfs = [mkres(KC) for _ in range(2)]
        yo = [sb(es, nc, "fyo%d" % i, [128, 2, D], F32) for i in range(2)]
        R_yo = mkres(2)
        sq = [sb(es, nc, "fsq%d" % i, [128, 512], BF16) for i in range(2)]
        R_sq = mkres(2)
        rstds = [sb(es, nc, "frstd%d" % i, [128, 512], F32) for i in range(2)]
        sds = [sb(es, nc, "fsd%d" % i, [128, 512], F32) for i in range(2)]
        R_rstds, R_sds = mkres(2), mkres(2)
        ch_x = [P.chan(), P.chan()]
        ch_o = [P.chan("pool"), P.chan("pool")]
        nt = seq.nt
        oc = 0
        for i in range(seq.ntiles):
            sx = i % 2
            yf, R_yf = yfs[sx], R_yfs[sx]
            rstd, R_rstd, sd, R_sd = rstds[sx], R_rstds[sx], sds[sx], R_sds[sx]
            P.dma("sp", ch_x[sx], xt[sx][:, :, :nt], seq.tile_ap(i), reads=[seq.res[i]], writes=R_x[sx])
            emit_norm_stats(P, g, xt[sx], R_x[sx], nt, sq, R_sq, 0, rstd, R_rstd, sd, R_sd)
            for c in range(KC):
                P.op("dve", lambda e, c=c, sx=sx: e.scalar_tensor_tensor(
                    out=yf[:, c, :nt], in0=xt[sx][:, c, :nt], scalar=g.gT[:, 3 * L, c:c + 1], in1=rstd[:, :nt],
                    op0=ALU.mult, op1=ALU.mult), reads=[R_x[sx][c], R_rstd, g.R_mods], writes=[R_yf[c]])
            for qq in range(nt // 256):
                so = oc % 2
                oc += 1
                for h in range(2):
                    q = qq * 2 + h
                    for c in range(KC):
                        bank = 1 + h * 2 + c // 4
                        P.op("pe", lambda e, c=c, q=q, bank=bank: e.transpose(
                            out=g.ps[bank][:, (c % 4) * 128:(c % 4 + 1) * 128], in_=yf[:, c, q * 128:(q + 1) * 128],
                            identity=g.ident[:, :]), reads=[R_yf[c]], writes=[g.Rps[bank]])
                    for hh in range(2):
                        bank = 1 + h * 2 + hh
                        if hh == 0:
                            P.op("dve", lambda e, h=h, hh=hh, bank=bank, so=so: e.tensor_copy(
                                out=yo[so][:, h, hh * 512:(hh + 1) * 512], in_=g.ps[bank][:, :]),
                                reads=[g.Rps[bank]], writes=[R_yo[so]])
                        else:
                            P.op("act", lambda e, h=h, hh=hh, bank=bank, so=so: e.copy(
                                out=yo[so][:, h, hh * 512:(hh + 1) * 512], in_=g.ps[bank][:, :]),
                                reads=[g.Rps[bank]], writes=[R_yo[so]])
                r0 = i * nt + qq * 256
                P.dma("pool", ch_o[so], dst[r0:r0 + 256, :].rearrange("(h p) f -> p h f", p=128), yo[so][:, :, :],
                      reads=[R_yo[so]], writes=[g.R_out])
        P.barrier()


def ffn_phase(P, g, jobs):
    nc = P.nc
    GW = 384
    groups = []
    c0 = 0
    while c0 < DFF:
        c1 = min(c0 + GW, DFF)
        groups.append((c0, c1))
        c0 = c1
    NG = len(groups)
    tiles = []
    sup_idx = []
    for jb, (l_, n_, _, _, supers) in enumerate(jobs):
        for S in supers:
            ids = []
            for jj, (seq, ti) in enumerate(S):
                ids.append(len(tiles))
                tiles.append((seq, ti, jj, jb))
            sup_idx.append((jb, ids))
    NT = len(tiles)
    NS = len(sup_idx)
    with ExitStack() as es:
        act = sb(es, nc, "f_act", [128, FCH, 1024], BF16)
        hT = sb(es, nc, "f_hT", [128, KC, 1024], BF16)
        wd = sb(es, nc, "f_wd", [128, FCH, D], BF16)
        xt = [sb(es, nc, "f_xt%d" % i, [128, KC, 512], F32) for i in range(3)]
        wgu = [sb(es, nc, "f_wgu%d" % i, [128, KC, 2, GW], BF16) for i in range(2)]
        sg = [sb(es, nc, "f_sg%d" % i, [128, 512], F32) for i in range(2)]
        W = alloc_norm_work(es, nc, 0)
        R_act = [mkres(2) for _ in range(FCH)]
        R_h = [mkres(KC) for _ in range(2)]
        R_x = [mkres(KC) for _ in range(3)]
        R_wd = mkres(4)
        R_wgu = mkres(2)
        R_sg = mkres(2)
        ch_x = [P.chan() for _ in range(3)]
        ch_st = [P.chan() for _ in range(3)]
        ch_wgu = [P.chan("pool"), P.chan("pool")]
        ch_wd = [P.chan("pool") for _ in range(4)]
        st = dict(gi=0, pcnt=0, ocnt=0)

        def load_group(jb, gidx, slot):
            l_, n_, d_wgu, d_wdown, _ = jobs[jb]
            a0, a1 = groups[gidx]
            w = a1 - a0
            P.dma("pool", ch_wgu[slot], wgu[slot][:, :, 0, 0:w],
                  d_wgu[l_, :, a0:a1].rearrange("(k p) n -> p k n", p=128), writes=[R_wgu[slot]])
            P.dma("pool", ch_wgu[slot], wgu[slot][:, :, 1, 0:w],
                  d_wgu[l_, :, DFF + a0:DFF + a1].rearrange("(k p) n -> p k n", p=128), writes=[R_wgu[slot]])

        def load_wdown(jb, qs=(0, 1, 2, 3)):
            l_, n_, d_wgu, d_wdown, _ = jobs[jb]
            for q in qs:
                P.dma("pool", ch_wd[q], wd[:, 0:FCH - 1, q * 256:(q + 1) * 256],
                      d_wdown[l_, 0:(FCH - 1) * 128, q * 256:(q + 1) * 256].rearrange("(c p) d -> p c d", p=128),
                      writes=[R_wd[q]])
                P.dma("pool", ch_wd[q], wd[0:64, FCH - 1, q * 256:(q + 1) * 256],
                      d_wdown[l_, (FCH - 1) * 128:DFF, q * 256:(q + 1) * 256], writes=[R_wd[q]])

        def load(t):
            seq, ti, jj, jb = tiles[t]
            s3 = t % 3
            P.dma("sp", ch_x[s3], xt[s3][:, :, :seq.nt], seq.tile_ap(ti), reads=[seq.res[ti]], writes=R_x[s3])

        def norm(t):
            seq, ti, jj, jb = tiles[t]
            s3 = t % 3
            emit_modnorm(P, g, jobs[jb][0], jobs[jb][1], seq.s, xt[s3], R_x[s3], seq.nt, hT, jj * 512, R_h[jj], W,
                         sq_in_h=True)

        def store(t):
            seq, ti, jj, jb = tiles[t]
            s3 = t % 3
            P.dma("sp", ch_st[s3], seq.tile_ap(ti), xt[s3][:, :, :seq.nt], reads=R_x[s3], writes=[seq.res[ti]])

        normed = set()

        def norm_stage(t, stage):
            seq, ti, jj, jb = tiles[t]
            l_, n_ = jobs[jb][0], jobs[jb][1]
            s3 = t % 3
            nt = seq.nt
            off = jj * 512
            if stage == 0:
                for c in range(KC):
                    P.op("act", lambda e, c=c: e.activation(out=hT[:, c, off:off + nt], in_=xt[s3][:, c, :nt],
                                                            func=AF.Square), reads=[R_x[s3][c]], writes=[R_h[jj][c]])
            elif stage == 1:
                for c in range(KC):
                    P.op("pe", lambda e, c=c: e.matmul(g.ps[0][:, :nt], lhsT=g.ones_dm[:, :], rhs=hT[:, c, off:off + nt],
                                                       start=(c == 0), stop=(c == KC - 1)),
                         reads=[R_h[jj][c]], writes=[g.Rps[0]])
                P.op("act", lambda e: e.activation(out=W["sd"][:, :nt], in_=g.ps[0][:, :nt], func=AF.Ln,
                                                   bias=g.eps_t[:, 0:1], scale=1.0), reads=[g.Rps[0]], writes=[W["R_sd"]])
                P.op("act", lambda e: e.activation(out=W["rstd"][:, :nt], in_=W["sd"][:, :nt], func=AF.Exp, scale=-0.5),
                     reads=[W["R_sd"]], writes=[W["R_rstd"]])
            else:
                for c in range(KC):
                    ts = c % 2
                    tb_ = W["t"][ts]
                    P.op("dve", lambda e, c=c: e.scalar_tensor_tensor(
                        out=tb_[:, :nt], in0=xt[s3][:, c, :nt], scalar=g.GS[:, l_, n_, c, seq.s:seq.s + 1],
                        in1=W["rstd"][:, :nt], op0=ALU.mult, op1=ALU.mult),
                        reads=[R_x[s3][c], W["R_rstd"], g.R_mods], writes=[W["R_t"][ts]])
                    P.op("act", lambda e, c=c: e.activation(
                        out=hT[:, c, off:off + nt], in_=tb_[:, :nt], func=AF.Identity,
                        bias=shift_ap(g, l_, n_, c, seq.s), scale=1.0),
                        reads=[W["R_t"][ts], g.R_mods], writes=[R_h[jj][c]])
                normed.add(t)

        def phaseA(t, slot, a0, a1):
            seq, ti, jj, jb = tiles[t]
            nt = seq.nt
            off = jj * 512
            for fo in range(0, a1 - a0, 128):
                fw = min(128, a1 - a0 - fo)
                f = (a0 + fo) // 128
                pb = st["pcnt"] % 2
                st["pcnt"] += 1
                gb, ub = 1 + pb, 3 + pb
                for k in range(KC):
                    P.op("pe", lambda e, k=k: e.matmul(
                        g.ps[gb][:fw, :nt], lhsT=wgu[slot][:, k, 0, fo:fo + fw],
                        rhs=hT[:, k, off:off + nt], start=(k == 0), stop=(k == KC - 1)),
                        reads=[R_wgu[slot], R_h[jj][k]], writes=[g.Rps[gb]])
                for k in range(KC):
                    P.op("pe", lambda e, k=k: e.matmul(
                        g.ps[ub][:fw, :nt], lhsT=wgu[slot][:, k, 1, fo:fo + fw],
                        rhs=hT[:, k, off:off + nt], start=(k == 0), stop=(k == KC - 1)),
                        reads=[R_wgu[slot], R_h[jj][k]], writes=[g.Rps[ub]])
                P.op("act", lambda e: e.activation(out=sg[pb][:fw, :nt], in_=g.ps[gb][:fw, :nt], func=AF.Silu),
                     reads=[g.Rps[gb]], writes=[R_sg[pb]])
                P.op("dve", lambda e: e.tensor_tensor(
                    out=act[:fw, f, off:off + nt], in0=sg[pb][:fw, :nt], in1=g.ps[ub][:fw, :nt],
                    op=ALU.mult), reads=[R_sg[pb], g.Rps[ub]], writes=[R_act[f][jj]])

        def phaseB(t, hook=None):
            seq, ti, jj, jb = tiles[t]
            l_, n_ = jobs[jb][0], jobs[jb][1]
            nt = seq.nt
            off = jj * 512
            s3 = t % 3
            for dch in range(KC):
                ob = 5 + (st["ocnt"] % 2)
                st["ocnt"] += 1
                for f in range(FCH):
                    fw = 128 if f < FCH - 1 else 64
                    P.op("pe", lambda e, f=f, fw=fw: e.matmul(
                        g.ps[ob][:, :nt], lhsT=wd[:fw, f, dch * 128:(dch + 1) * 128],
                        rhs=act[:fw, f, off:off + nt], start=(f == 0), stop=(f == FCH - 1)),
                        reads=[R_wd[dch // 2], R_act[f][jj]], writes=[g.Rps[ob]])
                P.op("dve", lambda e: e.scalar_tensor_tensor(
                    out=xt[s3][:, dch, :nt], in0=g.ps[ob][:, :nt], scalar=g.GT[:, l_, n_, dch, seq.s:seq.s + 1],
                    in1=xt[s3][:, dch, :nt], op0=ALU.mult, op1=ALU.add),
                    reads=[g.Rps[ob], R_x[s3][dch], g.R_mods], writes=[R_x[s3][dch]])
                if hook is not None and dch in hook:
                    hook[dch]()

        pend_wd = [(0, (q,)) for q in range(4)]
        load(0)
        load_group(0, 0, 0)
        norm(0)
        for t in range(1, min(3, NT)):
            load(t)
        for si, (jb, ids) in enumerate(sup_idx):
            nxt = sup_idx[si + 1][1] if si + 1 < NS else []
            for gidx in range(NG):
                slot = st["gi"] % 2
                st["gi"] += 1
                if gidx + 1 < NG:
                    load_group(jb, gidx + 1, (slot + 1) % 2)
                elif si + 1 < NS:
                    load_group(sup_idx[si + 1][0], 0, (slot + 1) % 2)
                if pend_wd:
                    load_wdown(*pend_wd.pop(0))
                a0, a1 = groups[gidx]
                phaseA(ids[0], slot, a0, a1)
                if len(ids) == 2:
                    if gidx == 0 and ids[1] not in normed:
                        norm(ids[1])
                        normed.add(ids[1])
                    phaseA(ids[1], slot, a0, a1)
            if len(ids) == 2:
                phaseB(ids[0])
                store(ids[0])
                if len(nxt) == 2:
                    load(nxt[1])
                hk = {}
                if nxt:
                    hk[0] = lambda: norm_stage(nxt[0], 0)
                    hk[1] = lambda: norm_stage(nxt[0], 1)
                    hk[2] = lambda: norm_stage(nxt[0], 2)
                    if len(nxt) == 2:
                        hk[4] = lambda: norm_stage(nxt[1], 0)
                        hk[5] = lambda: norm_stage(nxt[1], 1)
                        hk[6] = lambda: norm_stage(nxt[1], 2)
                phaseB(ids[1], hook=hk)
                store(ids[1])
                if si + 2 < NS:
                    load(sup_idx[si + 2][1][0])
            else:
                hk = {}
                if nxt:
                    hk[1] = lambda: norm_stage(nxt[0], 0)
                    hk[2] = lambda: norm_stage(nxt[0], 1)
                    hk[3] = lambda: norm_stage(nxt[0], 2)
                phaseB(ids[0], hook=hk)
                store(ids[0])
                if len(nxt) == 2:
                    load(nxt[1])
                if si + 2 < NS:
                    load(sup_idx[si + 2][1][0])
            if si + 1 < NS and sup_idx[si + 1][0] != jb:
                for q in range(4):
                    pend_wd.append((sup_idx[si + 1][0], (q,)))
        P.barrier()


def make_supers(seqx, seqc):
    supers = []
    for i in range(0, seqx.ntiles, 2):
        supers.append([(seqx, i + k) for k in range(min(2, seqx.ntiles - i))])
    if seqc is not None:
        supers.append([(seqc, 0)])
    return supers


def even_mixer(P, g, l, seqs):
    nc = P.nc
    j = l // 2
    with ExitStack() as es:
        wout = sb(es, nc, "e_wout", [128, KC, D], BF16)
        R_w = Res()
        chw = P.chan("pool")
        chw2 = P.chan("pool")
        wout_loaded = [False]
        for seq in seqs:
            T = seq.T
            nt = seq.nt
            ntk = T // 128
            with ExitStack() as es2:
                a_tok = sb(es2, nc, "e_atok", [128, ntk, 512], BF16)
                sgu = sb(es2, nc, "e_sgu", [128, 4, T], BF16)
                R_atok = mkres(ntk)
                R_sgu = mkres(seq.ntiles)
                with ExitStack() as es3:
                    win = sb(es3, nc, "e_win", [128, KC, 1536], BF16)
                    R_win = Res()
                    P.dma("pool", chw2, win[:, :, :], g.d_win[j, :, :].rearrange("(k p) n -> p k n", p=128),
                          writes=[R_win])
                    wsf = sb(es3, nc, "e_wsf", [128, 4, 128], F32)
                    wsT = sb(es3, nc, "e_wsT", [128, 4, 128], BF16)
                    gv_bc = sb(es3, nc, "e_gv", [128, 512], F32)
                    bs_bc = sb(es3, nc, "e_bs", [128, 4, 128], F32)
                    chs = P.chan()
                    R_s = Res()
                    P.dma("sp", chs, wsf[:, :, :], g.d_ws[j, :, :, :].rearrange("h p q -> p h q"), writes=[R_s])
                    P.dma("sp", chs, gv_bc[:, :], g.d_gv[j, :, :].rearrange("h c -> (h c)").partition_broadcast(128),
                          writes=[R_s])
                    P.dma("sp", chs, bs_bc[:, :, :].rearrange("p h q -> p (h q)"),
                          g.d_bs[j, :, :].rearrange("h c -> (h c)").partition_broadcast(128), writes=[R_s])
                    for h in range(4):
                        P.op("pe", lambda e, h=h: e.transpose(out=g.ps[7][:, h * 128:(h + 1) * 128], in_=wsf[:, h, :],
                                                              identity=g.ident[:, :]), reads=[R_s], writes=[g.Rps[7]])
                    P.op("dve", lambda e: e.tensor_copy(out=wsT[:, :, :].rearrange("p h q -> p (h q)"), in_=g.ps[7][:, :]),
                         reads=[g.Rps[7]], writes=[R_s])
                    xt1 = sb(es3, nc, "e_xt1", [128, KC, 512], F32)
                    R_x1 = mkres(KC)
                    ch_x1 = P.chan()
                    hT = [sb(es3, nc, "e_hT%d" % i, [128, KC, 512], BF16) for i in range(2)]
                    R_h = [mkres(KC) for _ in range(2)]
                    uT = [sb(es3, nc, "e_uT%d" % i, [128, 4, 512], F32) for i in range(2)]
                    R_u = [mkres(4) for _ in range(2)]
                    vg = [sb(es3, nc, "e_vg%d" % i, [128, 512], F32) for i in range(2)]
                    vsq = [sb(es3, nc, "e_vsq%d" % i, [128, 512], F32) for i in range(2)]
                    vn = [sb(es3, nc, "e_vn%d" % i, [128, 512], BF16) for i in range(2)]
                    ssq = [sb(es3, nc, "e_ssq%d" % i, [128, 4], F32) for i in range(2)]
                    rsv = [sb(es3, nc, "e_rsv%d" % i, [128, 4], F32) for i in range(2)]
                    tmpm = [sb(es3, nc, "e_tmpm%d" % i, [128, 4, 128], F32) for i in range(2)]
                    R_vg, R_vsq, R_vn, R_ssq, R_rsv, R_tmpm = (mkres(2) for _ in range(6))
                    W = alloc_norm_work(es3, nc, 0)
                    nq = nt // 128

                    def m1_load_norm(ti):
                        sx = ti % 2
                        P.dma("sp", ch_x1, xt1[:, :, :nt], seq.tile_ap(ti), reads=[seq.res[ti]], writes=R_x1)
                        emit_modnorm(P, g, l, 1, seq.s, xt1, R_x1, nt, hT[sx], 0, R_h[sx], W, sq_in_h=True)

                    def m1_u(ti):
                        sx = ti % 2
                        for h in range(4):
                            bank = 1 + h % 2
                            for k in range(KC):
                                P.op("pe", lambda e, k=k: e.matmul(
                                    g.ps[bank][:, :nt], lhsT=win[:, k, 512 + h * 128:512 + (h + 1) * 128],
                                    rhs=hT[sx][:, k, :nt], start=(k == 0), stop=(k == KC - 1)),
                                    reads=[R_win, R_h[sx][k]], writes=[g.Rps[bank]])
                            P.op("act", lambda e: e.activation(
                                out=uT[sx][:, h, :nt], in_=g.ps[bank][:, :nt], func=AF.Gelu_apprx_tanh),
                                reads=[g.Rps[bank]], writes=[R_u[sx][h]])

                    def m1_proj(ti, q, c):
                        sx = ti % 2
                        tk = ti * nq + q
                        for k in range(KC):
                            P.op("pe", lambda e, k=k: e.matmul(
                                g.ps[3][:, :], lhsT=hT[sx][:, k, q * 128:(q + 1) * 128], rhs=win[:, k, 0:512],
                                start=(k == 0), stop=(k == KC - 1)),
                                reads=[R_win, R_h[sx][k]], writes=[g.Rps[3]])
                        P.op("act", lambda e: e.copy(out=a_tok[:, tk, :], in_=g.ps[3][:, :]),
                             reads=[g.Rps[3]], writes=[R_atok[tk]])
                        vb = 4 if c == 0 else 6
                        for k in range(KC):
                            P.op("pe", lambda e, k=k: e.matmul(
                                g.ps[vb][:, :], lhsT=hT[sx][:, k, q * 128:(q + 1) * 128], rhs=win[:, k, 1024:1536],
                                start=(k == 0), stop=(k == KC - 1)),
                                reads=[R_win, R_h[sx][k]], writes=[g.Rps[vb]])
                        P.op("act", lambda e: e.activation(out=vg[c][:, :], in_=g.ps[vb][:, :], func=AF.Gelu_apprx_tanh),
                             reads=[g.Rps[vb]], writes=[R_vg[c]])
                        P.op("dve", lambda e: e.tensor_tensor(out=vsq[c][:, :], in0=vg[c][:, :], in1=vg[c][:, :], op=ALU.mult),
                             reads=[R_vg[c]], writes=[R_vsq[c]])
                        P.op("dve", lambda e: e.tensor_reduce(
                            out=ssq[c][:, :], in_=vsq[c][:, :].rearrange("p (a b) -> p a b", a=4), axis=AX.X, op=ALU.add),
                            reads=[R_vsq[c]], writes=[R_ssq[c]])
                        P.op("act", lambda e: e.activation(out=ssq[c][:, :], in_=ssq[c][:, :], func=AF.Sqrt,
                                                           bias=g.eps_t[:, 0:1], scale=1.0 / 128),
                             reads=[R_ssq[c]], writes=[R_ssq[c]])
                        P.op("dve", lambda e: e.reciprocal(out=rsv[c][:, :], in_=ssq[c][:, :]),
                             reads=[R_ssq[c]], writes=[R_rsv[c]])
                        P.op("dve", lambda e: e.tensor_tensor(
                            out=vsq[c][:, :].rearrange("p (a b) -> p a b", a=4),
                            in0=vg[c][:, :].rearrange("p (a b) -> p a b", a=4),
                            in1=rsv[c][:, :].unsqueeze(2).broadcast_to([128, 4, 128]), op=ALU.mult),
                            reads=[R_vg[c], R_rsv[c]], writes=[R_vsq[c]])
                        P.op("dve", lambda e: e.tensor_tensor(out=vn[c][:, :], in0=vsq[c][:, :], in1=gv_bc[:, :], op=ALU.mult),
                             reads=[R_vsq[c], R_s], writes=[R_vn[c]])

                    def m1_sgu(ti, q, c):
                        sx = ti % 2
                        tk = ti * nq + q
                        mb = 5 if c == 0 else 7
                        for h in range(4):
                            P.op("pe", lambda e, h=h: e.matmul(
                                g.ps[mb][:, h * 128:(h + 1) * 128], lhsT=vn[c][:, h * 128:(h + 1) * 128], rhs=wsT[:, h, :],
                                start=True, stop=True), reads=[R_vn[c], R_s], writes=[g.Rps[mb]])
                        P.op("dve", lambda e: e.tensor_tensor(
                            out=tmpm[c][:, :, :], in0=g.ps[mb][:, :].rearrange("p (h q) -> p h q", h=4), in1=bs_bc[:, :, :],
                            op=ALU.add), reads=[g.Rps[mb], R_s], writes=[R_tmpm[c]])
                        P.op("dve", lambda e: e.tensor_tensor(
                            out=sgu[:, :, tk * 128:(tk + 1) * 128], in0=tmpm[c][:, :, :],
                            in1=uT[sx][:, :, q * 128:(q + 1) * 128], op=ALU.mult),
                            reads=[R_tmpm[c]] + R_u[sx], writes=[R_sgu[ti]])

                    m1_load_norm(0)
                    if not wout_loaded[0]:
                        P.dma("pool", chw, wout[:, :, :], g.d_wout[j, :, :].rearrange("(k p) n -> p k n", p=128),
                              writes=[R_w])
                        wout_loaded[0] = True
                    pend = None
                    cc = 0
                    for ti in range(seq.ntiles):
                        m1_u(ti)
                        for q in range(nq):
                            c = cc % 2
                            cc += 1
                            m1_proj(ti, q, c)
                            if "nolag" in SKIP:
                                m1_sgu(ti, q, c)
                            else:
                                if pend is not None:
                                    m1_sgu(*pend)
                                pend = (ti, q, c)
                            if q == 0 and ti + 1 < seq.ntiles:
                                m1_load_norm(ti + 1)
                    if pend is not None:
                        m1_sgu(*pend)
                P.barrier()
                NB = nt
                with ExitStack() as es3:
                    xt = [sb(es3, nc, "e_xt%d" % i, [128, KC, 512], F32) for i in range(2)]
                    R_x = [mkres(KC) for _ in range(2)]
                    ch_x = [P.chan(), P.chan()]
                    ch_st = [P.chan("pool"), P.chan("pool")]
                    cn = [sb(es3, nc, "e_cn%d" % i, [128, ntk, NB], BF16) for i in range(2)]
                    R_cn = mkres(2)
                    ch_cn = [P.chan(), P.chan()]
                    fT = sb(es3, nc, "e_fT", [128, 4, NB], BF16)
                    R_f = mkres(4)
                    d_cs = g.d_dftx if seq.s == 0 else g.d_dftctx
                    nblk = T // NB
                    sym = (seq.s == 0) and (nblk >= 2) and (nblk % 2 == 0) and ("nosym" not in SKIP)
                    if not sym:
                        y12 = [sb(es3, nc, "e_y%d" % i, [128, NB], BF16) for i in range(2)]
                        R_y = mkres(2)

                    def load_cs(b, cs, slot):
                        P.dma("sp", ch_cn[slot], cn[slot][:, :, :],
                              d_cs[cs, :, b * NB:(b + 1) * NB].rearrange("(t p) n -> p t n", p=128), writes=[R_cn[slot]])
                    load_cs(0, 0, 0)
                    load_cs(0, 1, 1)
                    if sym:
                        nd = nblk // 2
                        fTm = [sb(es3, nc, "e_fTm%d" % i, [128, 4, NB], BF16) for i in range(2)]
                        R_fm = [mkres(4) for _ in range(2)]
                        ynq = sb(es3, nc, "e_ynq", [128, 4], BF16)
                        R_ynq = Res()
                        ocnt = [0]

                        def out_proj_tile(tile, fbuf, R_fb):
                            sx = ocnt[0] % 2
                            ocnt[0] += 1
                            P.dma("sp", ch_x[sx], xt[sx][:, :, :NB], seq.tile_ap(tile), reads=[seq.res[tile]], writes=R_x[sx])
                            for dch in range(KC):
                                ob = 4 + dch % 2
                                for k in range(KC):
                                    if k < 4:
                                        P.op("pe", lambda e, k=k: e.matmul(
                                            g.ps[ob][:, :NB], lhsT=wout[:, k, dch * 128:(dch + 1) * 128], rhs=fbuf[:, k, :],
                                            start=(k == 0), stop=False), reads=[R_w, R_fb[k]], writes=[g.Rps[ob]])
                                    else:
                                        P.op("pe", lambda e, k=k: e.matmul(
                                            g.ps[ob][:, :NB], lhsT=wout[:, k, dch * 128:(dch + 1) * 128],
                                            rhs=sgu[:, k - 4, tile * NB:(tile + 1) * NB], start=False, stop=(k == KC - 1)),
                                            reads=[R_w, R_sgu[tile]], writes=[g.Rps[ob]])
                                P.op("dve", lambda e: e.scalar_tensor_tensor(
                                    out=xt[sx][:, dch, :NB], in0=g.ps[ob][:, :NB], scalar=g.GT[:, l, 1, dch, seq.s:seq.s + 1],
                                    in1=xt[sx][:, dch, :NB], op0=ALU.mult, op1=ALU.add),
                                    reads=[g.Rps[ob], R_x[sx][dch], g.R_mods], writes=[R_x[sx][dch]])
                            P.dma("pool", ch_st[sx], seq.tile_ap(tile), xt[sx][:, :, :NB], reads=R_x[sx], writes=[seq.res[tile]])

                        y1s = [sb(es3, nc, "e_y1_%d" % i, [128, NB], BF16) for i in range(4)]
                        R_y1 = mkres(4)
                        y2s = [sb(es3, nc, "e_y2_%d" % i, [128, NB], BF16) for i in range(2)]
                        R_y2 = mkres(2)
                        for b in range(nd):
                            ms = b % 2
                            for h in range(4):
                                for t in range(ntk):
                                    P.op("pe", lambda e, t=t: e.matmul(
                                        g.ps[1][:, :NB], lhsT=a_tok[:, t, h * 128:(h + 1) * 128], rhs=cn[0][:, t, :],
                                        start=(t == 0), stop=(t == ntk - 1)),
                                        reads=[R_atok[t], R_cn[0]], writes=[g.Rps[1]])
                                P.op("act", lambda e: e.copy(out=y1s[h][:, :], in_=g.ps[1][:, :NB]),
                                     reads=[g.Rps[1]], writes=[R_y1[h]])
                            if b + 1 < nd:
                                load_cs(b + 1, 0, 0)
                            for h in range(4):
                                ys = h % 2
                                for t in range(ntk):
                                    P.op("pe", lambda e, t=t: e.matmul(
                                        g.ps[2][:, :NB], lhsT=a_tok[:, t, h * 128:(h + 1) * 128], rhs=cn[1][:, t, :],
                                        start=(t == 0), stop=(t == ntk - 1)),
                                        reads=[R_atok[t], R_cn[1]], writes=[g.Rps[2]])
                                P.op("dve", lambda e: e.tensor_copy(out=y2s[ys][:, :], in_=g.ps[2][:, :NB]),
                                     reads=[g.Rps[2]], writes=[R_y2[ys]])
                                if h == 3 and b + 1 < nd:
                                    load_cs(b + 1, 1, 1)
                                P.op("pe", lambda e: e.matmul(g.ps[3][:, :NB], lhsT=g.dftc[:, 0, :], rhs=y1s[h][:, :],
                                                              start=True, stop=False), reads=[R_y1[h]], writes=[g.Rps[3]])
                                P.op("pe", lambda e: e.matmul(g.ps[3][:, :NB], lhsT=g.dftc[:, 1, :], rhs=y2s[ys][:, :],
                                                              start=False, stop=True), reads=[R_y2[ys]], writes=[g.Rps[3]])
                                P.op("act", lambda e: e.copy(out=fT[:, h, :], in_=g.ps[3][:, :NB]),
                                     reads=[g.Rps[3]], writes=[R_f[h]])
                                P.op("pe", lambda e: e.matmul(g.ps[6][:, :NB], lhsT=g.dftc[:, 0, :], rhs=y1s[h][:, :],
                                                              start=True, stop=False), reads=[R_y1[h]], writes=[g.Rps[6]])
                                P.op("pe", lambda e: e.matmul(g.ps[6][:, :NB], lhsT=g.dftc[:, 2, :], rhs=y2s[ys][:, :],
                                                              start=False, stop=True), reads=[R_y2[ys]], writes=[g.Rps[6]])
                                P.op("dve", lambda e: e.tensor_copy(out=fTm[ms][:, h, NB - 1:0:-1], in_=g.ps[6][:, 1:NB]),
                                     reads=[g.Rps[6]], writes=[R_fm[ms][h]])
                                if b >= 1:
                                    P.op("dve", lambda e: e.tensor_copy(out=fTm[1 - ms][:, h, 0:1], in_=g.ps[6][:, 0:1]),
                                         reads=[g.Rps[6]], writes=[R_fm[1 - ms][h]])
                            out_proj_tile(b, fT, R_f)
                            if b >= 1:
                                out_proj_tile(nblk - b, fTm[1 - ms], R_fm[1 - ms])
                        msl = (nd - 1) % 2
                        for h in range(4):
                            for t in range(ntk):
                                P.op("pe", lambda e, t=t: e.matmul(
                                    g.ps[7][:, h:h + 1], lhsT=a_tok[:, t, h * 128:(h + 1) * 128], rhs=g.nyq[:, t:t + 1],
                                    start=(t == 0), stop=(t == ntk - 1)), reads=[R_atok[t]], writes=[g.Rps[7]])
                        P.op("act", lambda e: e.copy(out=ynq[:, :], in_=g.ps[7][:, 0:4]), reads=[g.Rps[7]], writes=[R_ynq])
                        for h in range(4):
                            P.op("pe", lambda e: e.matmul(g.ps[7][:, 8 + h:9 + h], lhsT=g.dftc[:, 0, :], rhs=ynq[:, h:h + 1],
                                                          start=True, stop=True), reads=[R_ynq], writes=[g.Rps[7]])
                        for h in range(4):
                            P.op("dve", lambda e: e.tensor_copy(out=fTm[msl][:, h, 0:1], in_=g.ps[7][:, 8 + h:9 + h]),
                                 reads=[g.Rps[7]], writes=[R_fm[msl][h]])
                        out_proj_tile(nblk - nd, fTm[msl], R_fm[msl])
                    for b in (range(nblk) if not sym else []):
                        sx = b % 2
                        P.dma("sp", ch_x[sx], xt[sx][:, :, :NB], seq.tile_ap(b), reads=[seq.res[b]], writes=R_x[sx])
                        for h in range(4):
                            for cs in range(2):
                                bank = 1 + cs
                                for t in range(ntk):
                                    P.op("pe", lambda e, t=t, h=h, cs=cs, bank=bank: e.matmul(
                                        g.ps[bank][:, :NB], lhsT=a_tok[:, t, h * 128:(h + 1) * 128], rhs=cn[cs][:, t, :],
                                        start=(t == 0), stop=(t == ntk - 1)),
                                        reads=[R_atok[t], R_cn[cs]], writes=[g.Rps[bank]])
                                if cs == 0:
                                    P.op("act", lambda e, bank=bank: e.copy(out=y12[0][:, :], in_=g.ps[bank][:, :NB]),
                                         reads=[g.Rps[bank]], writes=[R_y[0]])
                                else:
                                    P.op("dve", lambda e, bank=bank: e.tensor_copy(out=y12[1][:, :], in_=g.ps[bank][:, :NB]),
                                         reads=[g.Rps[bank]], writes=[R_y[1]])
                                if h == 3 and b + 1 < nblk:
                                    load_cs(b + 1, cs, cs)
                            P.op("pe", lambda e: e.matmul(g.ps[3][:, :NB], lhsT=g.dftc[:, 0, :], rhs=y12[0][:, :],
                                                          start=True, stop=False), reads=[R_y[0]], writes=[g.Rps[3]])
                            P.op("pe", lambda e: e.matmul(g.ps[3][:, :NB], lhsT=g.dftc[:, 1, :], rhs=y12[1][:, :],
                                                          start=False, stop=True), reads=[R_y[1]], writes=[g.Rps[3]])
                            P.op("act", lambda e, h=h: e.copy(out=fT[:, h, :], in_=g.ps[3][:, :NB]),
                                 reads=[g.Rps[3]], writes=[R_f[h]])
                        for dch in range(KC):
                            ob = 4 + dch % 2
                            for k in range(KC):
                                if k < 4:
                                    P.op("pe", lambda e, k=k, dch=dch, ob=ob: e.matmul(
                                        g.ps[ob][:, :NB], lhsT=wout[:, k, dch * 128:(dch + 1) * 128], rhs=fT[:, k, :],
                                        start=(k == 0), stop=False), reads=[R_w, R_f[k]], writes=[g.Rps[ob]])
                                else:
                                    P.op("pe", lambda e, k=k, dch=dch, ob=ob, b=b: e.matmul(
                                        g.ps[ob][:, :NB], lhsT=wout[:, k, dch * 128:(dch + 1) * 128],
                                        rhs=sgu[:, k - 4, b * NB:(b + 1) * NB], start=False, stop=(k == KC - 1)),
                                        reads=[R_w, R_sgu[b]], writes=[g.Rps[ob]])
                            P.op("dve", lambda e, ob=ob, dch=dch, sx=sx: e.scalar_tensor_tensor(
                                out=xt[sx][:, dch, :NB], in0=g.ps[ob][:, :NB], scalar=g.GT[:, l, 1, dch, seq.s:seq.s + 1],
                                in1=xt[sx][:, dch, :NB], op0=ALU.mult, op1=ALU.add),
                                reads=[g.Rps[ob], R_x[sx][dch], g.R_mods], writes=[R_x[sx][dch]])
                        P.dma("pool", ch_st[sx], seq.tile_ap(b), xt[sx][:, :, :NB], reads=R_x[sx], writes=[seq.res[b]])
                P.barrier()


def odd_mixer(P, g, l, seqx, seqc, last):
    nc = P.nc
    j = l // 2
    T = seqx.T
    TK = T + TC
    nkt = TK // 128
    ntx = seqx.ntiles
    with ExitStack() as es:
        kT = sb(es, nc, "o_kT", [128, NKV, TK], BF16)
        vtok = sb(es, nc, "o_v", [128, nkt, NKV * HD], BF16)
        qT = sb(es, nc, "o_qT", [128, NH, TK], BF16)
        R_k = [mkres(ntx + 1) for _ in range(NKV)]
        R_v = mkres(nkt)
        R_q = [mkres(ntx + 1) for _ in range(NH)]
        with ExitStack() as es2:
            wqkv = sb(es2, nc, "o_wqkv", [128, KC, 1536], BF16)
            R_wq = Res()
            chw = P.chan("pool")
            P.dma("pool", chw, wqkv[:, :, :], g.d_wqkv[j, :, :].rearrange("(k p) n -> p k n", p=128), writes=[R_wq])
            xt = sb(es2, nc, "o_xt", [128, KC, 512], F32)
            R_x = mkres(KC)
            ch_x = P.chan()
            hT = sb(es2, nc, "o_hT", [128, KC, 512], BF16)
            R_h = mkres(KC)
            W = alloc_norm_work(es2, nc, 0)
            cs_t = [sb(es2, nc, "o_cs%d" % i, [128, 2, 512], F32) for i in range(2)]
            R_cs = mkres(2)
            ch_cs = [P.chan(), P.chan()]
            t1f = [sb(es2, nc, "o_t1f%d" % i, [128, 512], F32) for i in range(2)]
            t1b = [sb(es2, nc, "o_t1b%d" % i, [128, 512], BF16) for i in range(2)]
            hsq = [sb(es2, nc, "o_hsq%d" % i, [128, 512], BF16) for i in range(2)]
            hsd = [sb(es2, nc, "o_hsd%d" % i, [128, 512], F32) for i in range(2)]
            hrs = [sb(es2, nc, "o_hrs%d" % i, [128, 512], F32) for i in range(2)]
            ta = [sb(es2, nc, "o_ta%d" % i, [128, 512], F32) for i in range(2)]
            tb = [sb(es2, nc, "o_tb%d" % i, [128, 512], F32) for i in range(2)]
            R_t1f, R_t1b, R_hsq, R_hsd, R_hrs, R_ta, R_tb = (mkres(2) for _ in range(7))
            state = dict(hc=0)

            def head_proj(wc0, nt):
                hs = state["hc"] % 2
                state["hc"] += 1
                bank = 2 + hs
                for k in range(KC):
                    P.op("pe", lambda e, k=k: e.matmul(
                        g.ps[bank][:, :nt], lhsT=wqkv[:, k, wc0:wc0 + 128], rhs=hT[:, k, :nt],
                        start=(k == 0), stop=(k == KC - 1)), reads=[R_wq, R_h[k]], writes=[g.Rps[bank]])
                return hs

            def head_finish(gcol, hs, nt, rope, sx, dst_ap, Rdst):
                bank = 2 + hs
                P.op("act", lambda e: e.activation(out=hsq[hs][:, :nt], in_=g.ps[bank][:, :nt], func=AF.Square),
                     reads=[g.Rps[bank]], writes=[R_hsq[hs]])
                P.op("act", lambda e: e.activation(out=t1f[hs][:, :nt], in_=g.ps[bank][:, :nt], func=AF.Identity,
                                                   scale=g.gqk[:, gcol:gcol + 1]),
                     reads=[g.Rps[bank], g.R_mods], writes=[R_t1f[hs]])
                if rope:
                    P.op("act", lambda e: e.activation(out=t1b[hs][:, :nt], in_=g.ps[bank][:, :nt], func=AF.Identity,
                                                       scale=g.gqk[:, gcol:gcol + 1]),
                         reads=[g.Rps[bank], g.R_mods], writes=[R_t1b[hs]])
                sbank = 4 + hs
                P.op("pe", lambda e: e.matmul(g.ps[sbank][:, :nt], lhsT=g.ones_hd[:, :], rhs=hsq[hs][:, :nt],
                                              start=True, stop=True), reads=[R_hsq[hs]], writes=[g.Rps[sbank]])
                rbank = 6 + hs
                if rope:
                    P.op("pe", lambda e: e.matmul(g.ps[rbank][:, :nt], lhsT=g.rotm[:, :], rhs=t1b[hs][:, :nt],
                                                  start=True, stop=True), reads=[R_t1b[hs]], writes=[g.Rps[rbank]])
                P.op("act", lambda e: e.activation(out=hsd[hs][:, :nt], in_=g.ps[sbank][:, :nt], func=AF.Ln,
                                                   bias=g.eps_t[:, 0:1], scale=1.0),
                     reads=[g.Rps[sbank]], writes=[R_hsd[hs]])
                P.op("act", lambda e: e.activation(out=hrs[hs][:, :nt], in_=hsd[hs][:, :nt], func=AF.Exp, scale=-0.5),
                     reads=[R_hsd[hs]], writes=[R_hrs[hs]])
                if rope:
                    P.op("dve", lambda e: e.tensor_tensor(out=ta[hs][:, :nt], in0=t1f[hs][:, :nt],
                                                          in1=cs_t[sx][:, 0, :nt], op=ALU.mult),
                         reads=[R_t1f[hs], R_cs[sx]], writes=[R_ta[hs]])
                    P.op("dve", lambda e: e.tensor_tensor(out=tb[hs][:, :nt], in0=g.ps[rbank][:, :nt],
                                                          in1=cs_t[sx][:, 1, :nt], op=ALU.mult),
                         reads=[g.Rps[rbank], R_cs[sx]], writes=[R_tb[hs]])
                    P.op("dve", lambda e: e.tensor_tensor(out=ta[hs][:, :nt], in0=ta[hs][:, :nt], in1=tb[hs][:, :nt],
                                                          op=ALU.add), reads=[R_ta[hs], R_tb[hs]], writes=[R_ta[hs]])
                    P.op("dve", lambda e: e.tensor_tensor(out=dst_ap, in0=ta[hs][:, :nt], in1=hrs[hs][:, :nt],
                                                          op=ALU.mult), reads=[R_ta[hs], R_hrs[hs]], writes=[Rdst])
                else:
                    P.op("dve", lambda e: e.tensor_tensor(out=dst_ap, in0=t1f[hs][:, :nt], in1=hrs[hs][:, :nt],
                                                          op=ALU.mult), reads=[R_t1f[hs], R_hrs[hs]], writes=[Rdst])

            tiles = [(seqx, ti) for ti in range(ntx)] + [(seqc, 0)]
            for idx, (seq, ti) in enumerate(tiles):
                nt = seq.nt
                sx = idx % 2
                col0 = ti * nt if seq.s == 0 else T
                rope = seq.s == 0
                P.dma("sp", ch_x, xt[:, :, :nt], seq.tile_ap(ti), reads=[seq.res[ti]], writes=R_x)
                if rope:
                    P.dma("sp", ch_cs[sx], cs_t[sx][:, :, :nt],
                          g.d_rope[:, :, col0:col0 + nt].rearrange("a p n -> p a n"), writes=[R_cs[sx]])
                emit_modnorm(P, g, l, 1, seq.s, xt, R_x, nt, hT, 0, R_h, W)
                for q in range(nt // 128):
                    tk = col0 // 128 + q
                    for k in range(KC):
                        P.op("pe", lambda e, k=k, q=q: e.matmul(
                            g.ps[1][:, 0:256], lhsT=hT[:, k, q * 128:(q + 1) * 128], rhs=wqkv[:, k, 1280:1536],
                            start=(k == 0), stop=(k == KC - 1)), reads=[R_wq, R_h[k]], writes=[g.Rps[1]])
                    P.op("act", lambda e, tk=tk: e.copy(out=vtok[:, tk, :], in_=g.ps[1][:, 0:256]),
                         reads=[g.Rps[1]], writes=[R_v[tk]])
                hl = [(1024 + h * 128, 2 + j, kT[:, h, col0:col0 + nt], R_k[h][idx]) for h in range(NKV)]
                if not (last and seq.s == 1):
                    hl += [(h * 128, 4 + j, qT[:, h, col0:col0 + nt], R_q[h][idx]) for h in range(NH)]
                hs_next = head_proj(hl[0][0], nt)
                for hi, (wc0, gcol, dst_ap, Rdst) in enumerate(hl):
                    hs = hs_next
                    if hi + 1 < len(hl):
                        hs_next = head_proj(hl[hi + 1][0], nt)
                    head_finish(gcol, hs, nt, rope, sx, dst_ap, Rdst)
        P.barrier()
        with ExitStack() as es2:
            wo = sb(es2, nc, "o_wo", [128, KC, D], BF16)
            R_wo = Res()
            chw_o = P.chan("pool")
            P.dma("pool", chw_o, wo[:, :, :], g.d_wo[j, :, :].rearrange("(k p) n -> p k n", p=128), writes=[R_wo])
            xt = [sb(es2, nc, "o_xt%d" % i, [128, KC, 512], F32) for i in range(2)]
            R_x = [mkres(KC) for _ in range(2)]
            ch_x = [P.chan(), P.chan()]
            ch_st = [P.chan(), P.chan()]
            pT = [sb(es2, nc, "o_pT%d" % i, [128, 2, 512], BF16) for i in range(3)]
            R_p = [mkres(2) for _ in range(3)]
            acc = [sb(es2, nc, "o_acc%d" % i, [128, 2, 512], F32) for i in range(2)]
            R_acc = mkres(2)
            oT = [sb(es2, nc, "o_oT%d" % i, [128, NH, 512], BF16) for i in range(2)]
            R_o = [mkres(NH) for _ in range(2)]
            lden = sb(es2, nc, "o_lden", [128, 512], F32)
            rden = sb(es2, nc, "o_rden", [128, 512], F32)
            R_lden, R_rden = Res(), Res()
            state = dict(pc=0, hcnt=0)
            pend_head = []
            pend_proj = []
            qtiles = [(seqx, ti) for ti in range(ntx)] + ([] if last else [(seqc, 0)])
            if "oddA2" in SKIP:
                qtiles = []

            def load_x(idx):
                seq, ti = qtiles[idx]
                sx = idx % 2
                P.dma("sp", ch_x[sx], xt[sx][:, :, :seq.nt], seq.tile_ap(ti), reads=[seq.res[ti]], writes=R_x[sx])

            def finish_stage(stage, h, a, nt, obank, dbank, so):
                if stage == 0:
                    P.op("dve", lambda e: e.tensor_tensor(out=acc[a][:, 0, :nt], in0=acc[a][:, 0, :nt],
                                                          in1=acc[a][:, 1, :nt], op=ALU.add),
                         reads=[R_acc[a]], writes=[R_acc[a]])
                elif stage == 1:
                    P.op("pe", lambda e: e.matmul(g.ps[dbank][:, :nt], lhsT=g.ones_f[:, :], rhs=acc[a][:, 0, :nt],
                                                  start=True, stop=True), reads=[R_acc[a]], writes=[g.Rps[dbank]])
                elif stage == 2:
                    P.op("act", lambda e: e.activation(out=lden[:, :nt], in_=g.ps[dbank][:, :nt], func=AF.Ln),
                         reads=[g.Rps[dbank]], writes=[R_lden])
                    P.op("act", lambda e: e.activation(out=rden[:, :nt], in_=lden[:, :nt], func=AF.Exp, scale=-1.0),
                         reads=[R_lden], writes=[R_rden])
                else:
                    P.op("dve", lambda e: e.tensor_tensor(out=oT[so][:, h, :nt], in0=g.ps[obank][:, :nt],
                                                          in1=rden[:, :nt], op=ALU.mult),
                         reads=[g.Rps[obank], R_rden], writes=[R_o[so][h]])

            STAGE_AT = (5, 10, 14, 18)

            def advance_heads(ki):
                for p in list(pend_head):
                    while p["stage"] < 4 and (ki is None or ki >= STAGE_AT[p["stage"]]):
                        finish_stage(p["stage"], *p["args"])
                        p["stage"] += 1
                    if p["stage"] == 4:
                        pend_head.remove(p)

            def out_proj_chunk(idx, dch):
                seq, ti = qtiles[idx]
                nt = seq.nt
                sx = idx % 2
                so = idx % 2
                ob = 0 if dch % 2 == 0 else 7
                for k in range(KC):
                    P.op("pe", lambda e, k=k: e.matmul(
                        g.ps[ob][:, :nt], lhsT=wo[:, k, dch * 128:(dch + 1) * 128], rhs=oT[so][:, k, :nt],
                        start=(k == 0), stop=(k == KC - 1)), reads=[R_wo, R_o[so][k]], writes=[g.Rps[ob]])
                P.op("dve", lambda e: e.scalar_tensor_tensor(
                    out=xt[sx][:, dch, :nt], in0=g.ps[ob][:, :nt], scalar=g.GT[:, l, 1, dch, seq.s:seq.s + 1],
                    in1=xt[sx][:, dch, :nt], op0=ALU.mult, op1=ALU.add),
                    reads=[g.Rps[ob], R_x[sx][dch], g.R_mods], writes=[R_x[sx][dch]])
                if dch == KC - 1:
                    P.dma("sp", ch_st[sx], seq.tile_ap(ti), xt[sx][:, :, :nt], reads=R_x[sx], writes=[seq.res[ti]])

            def proj_step(n):
                done = False
                while n > 0 and pend_proj:
                    pidx, dch = pend_proj[0]
                    out_proj_chunk(pidx, dch)
                    if dch == KC - 1:
                        pend_proj.pop(0)
                        done = True
                    else:
                        pend_proj[0] = (pidx, dch + 1)
                    n -= 1
                return done

            if qtiles:
                load_x(0)
            for idx, (seq, ti) in enumerate(qtiles):
                nt = seq.nt
                so = idx % 2
                col0 = ti * nt if seq.s == 0 else T
                qidx = ti if seq.s == 0 else ntx
                next_loaded = False
                ktiles = list(range(nkt)) if seq.s == 0 else list(range(T // 128, nkt))
                nk = len(ktiles)
                for h in range(NH):
                    kvh = h // (NH // NKV)
                    hc = state["hcnt"]
                    state["hcnt"] += 1
                    obank, dbank = 4 + hc % 2, 6
                    a = hc % 2
                    base = state["pc"]

                    def emit_scores(ki):
                        kt = ktiles[ki]
                        gk = base + ki
                        sb_ = 1 + gk % 3
                        kidx = (kt * 128 // seqx.nt) if kt * 128 < T else ntx
                        P.op("pe", lambda e: e.matmul(
                            g.ps[sb_][:, :nt], lhsT=kT[:, kvh, kt * 128:(kt + 1) * 128], rhs=qT[:, h, col0:col0 + nt],
                            start=True, stop=True), reads=[R_k[kvh][kidx], R_q[h][qidx]], writes=[g.Rps[sb_]])
                    emit_scores(0)
                    if nk > 1:
                        emit_scores(1)
                    for ki, kt in enumerate(ktiles):
                        gk = base + ki
                        sb_ = 1 + gk % 3
                        slot, half = (gk // 2) % 3, gk % 2
                        if ki + 2 < nk:
                            emit_scores(ki + 2)
                        P.op("act", lambda e: e.activation(out=pT[slot][:, half, :nt], in_=g.ps[sb_][:, :nt], func=AF.Exp),
                             reads=[g.Rps[sb_]], writes=[R_p[slot][half]])
                        P.op("pe", lambda e: e.matmul(
                            g.ps[obank][:, :nt], lhsT=vtok[:, kt, kvh * 128:(kvh + 1) * 128], rhs=pT[slot][:, half, :nt],
                            start=(ki == 0), stop=(ki == nk - 1)), reads=[R_v[kt], R_p[slot][half]], writes=[g.Rps[obank]])
                        if half == 1:
                            if ki == 1:
                                P.op("dve", lambda e: e.tensor_copy(out=acc[a][:, :, :nt], in_=pT[slot][:, :, :nt]),
                                     reads=R_p[slot], writes=[R_acc[a]])
                            else:
                                P.op("dve", lambda e: e.tensor_tensor(out=acc[a][:, :, :nt], in0=acc[a][:, :, :nt],
                                                                      in1=pT[slot][:, :, :nt], op=ALU.add),
                                     reads=R_p[slot] + [R_acc[a]], writes=[R_acc[a]])
                        advance_heads(ki)
                        if ki == 23 and pend_proj:
                            proj_step(1)
                        if not pend_proj and not next_loaded and idx + 1 < len(qtiles) and (h > 0 or ki >= 16):
                            load_x(idx + 1)
                            next_loaded = True
                    state["pc"] += nk
                    advance_heads(None)
                    pend_head.append(dict(stage=0, args=(h, a, nt, obank, dbank, so)))
                    if nk <= STAGE_AT[-1]:
                        advance_heads(None)
                proj_step(10 ** 6)
                if idx + 1 < len(qtiles) and not next_loaded:
                    load_x(idx + 1)
                pend_proj.append((idx, 0))
            advance_heads(None)
            proj_step(10 ** 6)
        P.barrier()


def build_program(T, L=4, debug_stream=False):
    nc = bass.Bass("TRN2", target_bir_lowering=False)
    g = G()
    g.L = L
    n_even = (L + 1) // 2
    n_odd = max(L // 2, 1)

    def din(name, shape, dt=F32):
        return nc.dram_tensor(name, shape, dt, kind="ExternalInput").ap()
    g.d_x = din("x", [T, D])
    g.d_ctx = din("ctx", [TC, D])
    g.d_c = din("c", [1, D])
    g.d_cctx = din("c_ctx", [1, D])
    g.d_wmod = din("w_mod", [L, D, NMOD * D])
    g.d_bmod = din("b_mod", [1, L, NMOD * D])
    g.d_gffn1 = din("g_ffn1", [1, L, D])
    g.d_gmix = din("g_mix", [1, L, D])
    g.d_gffn2 = din("g_ffn2", [1, L, D])
    g.d_gfinal = din("g_final", [1, D])
    g.d_wgu1 = din("w_ffn1_gu", [L, D, 2 * DFF])
    g.d_wd1 = din("w_ffn1_down", [L, DFF, D])
    g.d_wgu2 = din("w_ffn2_gu", [L, D, 2 * DFF])
    g.d_wd2 = din("w_ffn2_down", [L, DFF, D])
    g.d_win = din("w_in_ab", [n_even, D, 1536])
    g.d_gv = din("g_v", [n_even, 4, 128])
    g.d_ws = din("w_s", [n_even, 4, 128, 128])
    g.d_bs = din("b_s", [n_even, 4, 128])
    g.d_wout = din("w_out_ab", [n_even, D, D])
    g.d_wqkv = din("w_qkv", [n_odd, D, 1536])
    g.d_gq = din("g_q", [1, n_odd, HD])
    g.d_gk = din("g_k", [1, n_odd, HD])
    g.d_wo = din("w_o", [n_odd, D, D])
    g.d_ident = din("k_ident", [128, 128])
    g.d_rot = din("k_rot", [128, 128])
    g.d_dftc = din("k_dftc", [3, 128, 128])
    g.d_nyq = din("k_nyq", [128, T // 128])
    g.d_dftx = din("k_dftx", [2, T, T], BF16)
    g.d_dftctx = din("k_dftctx", [2, TC, TC], BF16)
    g.d_rope = din("k_rope", [2, 128, T])
    y = nc.dram_tensor("y", [T, D], F32, kind="ExternalOutput").ap()
    xT = nc.dram_tensor("xT_scr", [D, T], F32).ap()
    cT = nc.dram_tensor("cT_scr", [D, TC], F32).ap()
    if debug_stream:
        dbg_x = nc.dram_tensor("dbg_x", [D, T], F32, kind="ExternalOutput").ap()
        dbg_c = nc.dram_tensor("dbg_c", [D, TC], F32, kind="ExternalOutput").ap()

    with ExitStack() as es:
        P = Prog(nc, es)
        g.ps = [es.enter_context(nc.psum_tensor("ps%d" % i, [128, 512], F32)) for i in range(8)]
        g.Rps = mkres(8)
        g.R_out = Res()
        g.R_mods = Res()
        g.ones_dm = sb(es, nc, "ones_dm", [128, 128], BF16)
        g.ones_hd = sb(es, nc, "ones_hd", [128, 128], BF16)
        g.ones_1 = sb(es, nc, "ones_1", [128, 128], BF16)
        g.one11 = sb(es, nc, "one11", [1, 2], F32)
        g.ones_f = sb(es, nc, "ones_f", [128, 128], F32)
        g.eps_t = sb(es, nc, "eps_t", [128, 1], F32)
        g.ident = sb(es, nc, "ident", [128, 128], F32)
        g.rotm_f = sb(es, nc, "rotm_f", [128, 128], F32)
        g.rotm = sb(es, nc, "rotm", [128, 128], BF16)
        g.dftc_f = sb(es, nc, "dftc_f", [128, 3, 128], F32)
        g.nyq_f = sb(es, nc, "nyq_f", [128, T // 128], F32)
        g.nyq = sb(es, nc, "nyq", [128, T // 128], BF16)
        g.dftc = sb(es, nc, "dftc", [128, 3, 128], BF16)
        g.mods = sb(es, nc, "mods", [128, L, NMOD * KC, 2], F32)
        g.gT = sb(es, nc, "gT", [128, 3 * L + 1, KC], F32)
        g.gqk = sb(es, nc, "gqk", [128, 8], F32)
        g.GS = sb(es, nc, "GS", [128, L, 3, KC, 2], F32)
        g.GT = sb(es, nc, "GT", [128, L, 3, KC, 2], F32)
        P.op("dve", lambda e: e.memset(g.eps_t[:, :], EPS), writes=[g.R_mods])

        seqx = Seq(xT, T, 0, 512)
        seqc = Seq(cT, TC, 1, 256)
        prologue(P, g)
        to_feature_major(P, g, g.d_x, seqx)
        to_feature_major(P, g, g.d_ctx, seqc)
        ffn_phase(P, g, [(0, 0, g.d_wgu1, g.d_wd1, make_supers(seqx, seqc))])
        for l in range(L):
            last = l == L - 1
            even = l % 2 == 0
            if even:
                if "even" not in SKIP:
                    even_mixer(P, g, l, [seqx] if (last or "evenctx" in SKIP) else [seqx, seqc])
            else:
                if "odd" not in SKIP:
                    odd_mixer(P, g, l, seqx, seqc, last)
            jobs = [(l, 2, g.d_wgu2, g.d_wd2, make_supers(seqx, None if last else seqc))]
            if not last:
                jobs.append((l + 1, 0, g.d_wgu1, g.d_wd1, make_supers(seqx, seqc)))
            ffn_phase(P, g, jobs)
        if debug_stream:
            chd = P.chan()
            P.dma("sp", chd, dbg_x[:, :], xT[:, :], reads=seqx.res, writes=[g.R_out])
            P.dma("sp", chd, dbg_c[:, :], cT[:, :], reads=seqc.res, writes=[g.R_out])
        final_phase(P, g, seqx, y)
    return nc


def host_constants(T):
    bf = ml_dtypes.bfloat16
    k = {}
    k["k_ident"] = np.eye(128, dtype=np.float32)
    rot = np.zeros((128, 128), np.float32)
    for base in (0, 64):
        for jj in range(32):
            rot[base + jj + 32, base + jj] = -1.0
            rot[base + jj, base + jj + 32] = 1.0
    k["k_rot"] = rot

    def dft(n, scale):
        idx = np.arange(n, dtype=np.int64)
        m = (idx[:, None] * idx[None, :]) % n
        ang = 2.0 * np.pi * m.astype(np.float64) / n
        return np.cos(ang) * scale, np.sin(ang) * scale
    cc, sc_ = dft(128, 128 ** -0.5)
    k["k_dftc"] = np.stack([cc, -sc_, sc_]).astype(np.float32)
    pp = np.arange(128)
    k["k_nyq"] = np.repeat((((-1.0) ** pp) * T ** -0.5)[:, None], T // 128, axis=1).astype(np.float32)
    cx, sx = dft(T, T ** -0.5)
    k["k_dftx"] = np.stack([cx.astype(np.float32), sx.astype(np.float32)]).astype(bf)
    del cx, sx
    c2, s2 = dft(TC, TC ** -0.5)
    k["k_dftctx"] = np.stack([c2.astype(np.float32), s2.astype(np.float32)]).astype(bf)
    rows = T // GRID_W
    row = np.repeat(np.arange(rows, dtype=np.float32), GRID_W)
    col = np.tile(np.arange(GRID_W, dtype=np.float32), rows)
    inv = (np.float32(10000.0) ** (-np.arange(0, 64, 2, dtype=np.float32) / np.float32(64))).astype(np.float32)
    ar = row[:, None] * inv[None, :]
    ac = col[:, None] * inv[None, :]
    full = np.concatenate([ar, ar, ac, ac], axis=1)
    k["k_rope"] = np.stack([np.cos(full).T, np.sin(full).T]).astype(np.float32)
    return k


_CACHE = {}


def make_in_maps(inputs, T, L, B):
    f = lambda a: np.ascontiguousarray(np.asarray(a, dtype=np.float32))
    shared = {
        "c_ctx": f(inputs["c_ctx"]).reshape(1, D),
        "w_mod": f(inputs["w_mod"])[:L],
        "b_mod": f(inputs["b_mod"])[:L].reshape(1, L, NMOD * D),
        "g_ffn1": f(inputs["g_ffn1"])[:L].reshape(1, L, D),
        "g_mix": f(inputs["g_mix"])[:L].reshape(1, L, D),
        "g_ffn2": f(inputs["g_ffn2"])[:L].reshape(1, L, D),
        "g_final": f(inputs["g_final"]).reshape(1, D),
        "w_ffn1_gu": f(inputs["w_ffn1_gu"])[:L],
        "w_ffn1_down": f(inputs["w_ffn1_down"])[:L],
        "w_ffn2_gu": f(inputs["w_ffn2_gu"])[:L],
        "w_ffn2_down": f(inputs["w_ffn2_down"])[:L],
        "w_in_ab": f(inputs["w_in_ab"]),
        "g_v": f(inputs["g_v"]),
        "w_s": f(inputs["w_s"]),
        "b_s": f(inputs["b_s"]),
        "w_out_ab": f(inputs["w_out_ab"]),
        "w_qkv": f(inputs["w_qkv"]),
        "g_q": f(inputs["g_q"]).reshape(1, -1, HD),
        "g_k": f(inputs["g_k"]).reshape(1, -1, HD),
        "w_o": f(inputs["w_o"]),
    }
    n_even = (L + 1) // 2
    n_odd = max(L // 2, 1)
    for nm in ("w_in_ab", "g_v", "w_s", "b_s", "w_out_ab"):
        shared[nm] = shared[nm][:n_even]
    for nm in ("w_qkv", "w_o"):
        shared[nm] = shared[nm][:n_odd]
    shared["g_q"] = shared["g_q"][:, :n_odd]
    shared["g_k"] = shared["g_k"][:, :n_odd]
    shared.update(host_constants(T))
    x = f(inputs["x"])
    ctx = f(inputs["ctx"])
    c = f(inputs["c"])
    maps = []
    for b in range(B):
        m = dict(shared)
        m["x"] = x[b]
        m["ctx"] = ctx[b]
        m["c"] = c[b].reshape(1, D)
        maps.append(m)
    return maps


def kernel(**inputs):
    x = np.asarray(inputs["x"])
    B, T, _ = x.shape
    L = 4
    key = (T, L)
    if key not in _CACHE:
        _CACHE[key] = build_program(T, L)
    nc = _CACHE[key]
    maps = make_in_maps(inputs, T, L, B)
    res = run_bass_kernel_spmd(nc, maps, core_ids=list(range(B)))
    out = np.stack([np.asarray(res.results[b]["y"], dtype=np.float32) for b in range(B)], axis=0)
    return out
```
